# Optimizing a Trainium2 kernel written in Bass

```python
import math
import jax, jax.numpy as jnp
from jax import lax
import numpy as np

D_MODEL = 1024
BATCH = 4
SEQ = 4096
DEPTH = 4

CHUNK = 64
Q_BLOCK = 128
N_A_LAYERS = DEPTH // 2
N_B_LAYERS = DEPTH - N_A_LAYERS
D_FF = 2816
D_RNN = D_MODEL
N_LRU_BLOCKS = 8
LRU_BLOCK = D_RNN // N_LRU_BLOCKS
CONV_WIDTH = 4
LRU_C = 8.0
N_HEADS = 8
HEAD_DIM = D_MODEL // (2 * N_HEADS)
ROPE_THETA = 10000.0
EPS = 1e-6
SUBLN_EPS = 1e-5

kernel_name = "hybrid_rglru_diffattn_yoco_macaron"


def rmsnorm(x, g, eps=EPS):
    xf = x.astype(jnp.float32)
    y = xf * lax.rsqrt(jnp.mean(xf * xf, axis=-1, keepdims=True) + eps)
    return (y * g.astype(jnp.float32)).astype(x.dtype)


def swiglu(h, w_gate, w_up, w_down):
    return (jax.nn.silu(h @ w_gate) * (h @ w_up)) @ w_down


def causal_depthwise_conv(x, w, b):
    S = x.shape[1]
    xp = jnp.pad(x, ((0, 0), (CONV_WIDTH - 1, 0), (0, 0)))
    out = b
    for k in range(CONV_WIDTH):
        out = out + xp[:, k:k + S, :] * w[k]
    return out


def _lru_combine(c1, c2):
    a1, b1 = c1
    a2, b2 = c2
    return a1 * a2, a2 * b1 + b2


def rg_lru(x, w_a, b_a, w_x, b_x, lam):
    B, S, _ = x.shape
    xb = x.reshape(B, S, N_LRU_BLOCKS, LRU_BLOCK)
    gate_a = jnp.einsum('bsnh,nhk->bsnk', xb, w_a).reshape(B, S, D_RNN) + b_a
    gate_x = jnp.einsum('bsnh,nhk->bsnk', xb, w_x).reshape(B, S, D_RNN) + b_x
    r = jax.nn.sigmoid(gate_a.astype(jnp.float32))
    i = jax.nn.sigmoid(gate_x.astype(jnp.float32))
    log_a = -LRU_C * r * jax.nn.softplus(-lam.astype(jnp.float32))
    a = jnp.exp(log_a)
    mult = jnp.sqrt(-jnp.expm1(2.0 * log_a))
    bx = mult * (i * x.astype(jnp.float32))
    _, h = lax.associative_scan(_lru_combine, (a, bx), axis=1)
    return h.astype(x.dtype)


def recurrent_block(h, w_in, conv_w, conv_b, w_a, b_a, w_x, b_x, lam, w_out):
    proj = h @ w_in
    gate, rec = proj[..., :D_RNN], proj[..., D_RNN:]
    rec = causal_depthwise_conv(rec, conv_w, conv_b)
    rec = rg_lru(rec, w_a, b_a, w_x, b_x, lam)
    return (jax.nn.gelu(gate) * rec) @ w_out


def rope_tables(S):
    pos = jnp.arange(S, dtype=jnp.float32)
    inv_freq = ROPE_THETA ** (-jnp.arange(0, HEAD_DIM, 2, dtype=jnp.float32) / HEAD_DIM)
    ang = pos[:, None] * inv_freq[None, :]
    ang = jnp.concatenate([ang, ang], axis=-1)
    return jnp.cos(ang), jnp.sin(ang)


def apply_rope(t, cos, sin):
    tf = t.astype(jnp.float32)
    half = HEAD_DIM // 2
    rot = jnp.concatenate([-tf[..., half:], tf[..., :half]], axis=-1)
    c = cos[None, :, None, None, :]
    s = sin[None, :, None, None, :]
    return (tf * c + rot * s).astype(t.dtype)


def shared_kv(x, kv_norm, w_k, w_v, k_norm, cos, sin):
    B, S, _ = x.shape
    h = rmsnorm(x, kv_norm)
    k = (h @ w_k).reshape(B, S, N_HEADS, 2, HEAD_DIM)
    k = apply_rope(rmsnorm(k, k_norm), cos, sin)
    k = k.transpose(0, 2, 3, 1, 4)
    v = (h @ w_v).reshape(B, S, N_HEADS, 2 * HEAD_DIM).transpose(0, 2, 1, 3)
    return k, v


def diff_attention(h, k, v, w_q, q_norm, lq1, lq2, lk1, lk2, sub_norm, w_o, cos, sin, lambda_init):
    B, S, _ = h.shape
    nb = S // Q_BLOCK
    q = (h @ w_q).reshape(B, S, N_HEADS, 2, HEAD_DIM)
    q = apply_rope(rmsnorm(q, q_norm), cos, sin) * (HEAD_DIM ** -0.5)
    q_blocks = q.reshape(B, nb, Q_BLOCK, N_HEADS, 2, HEAD_DIM).transpose(1, 0, 3, 4, 2, 5)
    lam = (jnp.exp(jnp.sum((lq1 * lk1).astype(jnp.float32)))
           - jnp.exp(jnp.sum((lq2 * lk2).astype(jnp.float32))) + lambda_init)
    key_chunk = jnp.arange(S) // CHUNK
    vf = v.astype(jnp.float32)

    def one_block(args):
        qb, bi = args
        scores = jnp.einsum('bhcqd,bhckd->bhcqk', qb, k, preferred_element_type=jnp.float32)
        q_chunk = (bi * Q_BLOCK + jnp.arange(Q_BLOCK)) // CHUNK
        mask = key_chunk[None, :] <= q_chunk[:, None]
        p = jax.nn.softmax(jnp.where(mask, scores, -jnp.inf), axis=-1)
        attn = p[:, :, 0] - lam * p[:, :, 1]
        return jnp.einsum('bhqk,bhkd->bhqd', attn, vf)

    o = lax.map(one_block, (q_blocks, jnp.arange(nb)))
    o = o.transpose(1, 0, 3, 2, 4).reshape(B, S, N_HEADS, 2 * HEAD_DIM)
    o = rmsnorm(o, sub_norm, SUBLN_EPS) * (1.0 - lambda_init)
    return o.reshape(B, S, D_MODEL).astype(h.dtype) @ w_o


def setup_inputs(seed: int = 0) -> dict:
    key = jax.random.key(seed)
    ks = iter(jax.random.split(key, 64))
    f32 = jnp.float32

    def dense(shape, fan_in):
        return jax.random.normal(next(ks), shape, f32) * (fan_in ** -0.5)

    def gain(shape):
        return 1.0 + 0.02 * jax.random.normal(next(ks), shape, f32)

    def bias(shape):
        return 0.01 * jax.random.normal(next(ks), shape, f32)

    nA, nB = N_A_LAYERS, N_B_LAYERS
    u = jax.random.uniform(next(ks), (nA, D_RNN), f32, 0.9, 0.999)
    a0 = u ** (1.0 / LRU_C)
    rec_lambda = jnp.log(a0) - jnp.log1p(-a0)
    return {
        'x': jax.random.normal(next(ks), (BATCH, SEQ, D_MODEL), f32),
        'ffn1_norm': gain((DEPTH, D_MODEL)),
        'ffn1_w_gate': dense((DEPTH, D_MODEL, D_FF), D_MODEL),
        'ffn1_w_up': dense((DEPTH, D_MODEL, D_FF), D_MODEL),
        'ffn1_w_down': dense((DEPTH, D_FF, D_MODEL), D_FF),
        'ffn2_norm': gain((DEPTH, D_MODEL)),
        'ffn2_w_gate': dense((DEPTH, D_MODEL, D_FF), D_MODEL),
        'ffn2_w_up': dense((DEPTH, D_MODEL, D_FF), D_MODEL),
        'ffn2_w_down': dense((DEPTH, D_FF, D_MODEL), D_FF),
        'mix_norm': gain((DEPTH, D_MODEL)),
        'rec_w_in': dense((nA, D_MODEL, 2 * D_RNN), D_MODEL),
        'rec_conv_w': dense((nA, CONV_WIDTH, D_RNN), CONV_WIDTH),
        'rec_conv_b': bias((nA, D_RNN)),
        'rec_w_a': dense((nA, N_LRU_BLOCKS, LRU_BLOCK, LRU_BLOCK), LRU_BLOCK),
        'rec_b_a': bias((nA, D_RNN)),
        'rec_w_x': dense((nA, N_LRU_BLOCKS, LRU_BLOCK, LRU_BLOCK), LRU_BLOCK),
        'rec_b_x': bias((nA, D_RNN)),
        'rec_lambda': rec_lambda,
        'rec_w_out': dense((nA, D_RNN, D_MODEL), D_RNN),
        'kv_norm': gain((D_MODEL,)),
        'w_k': dense((D_MODEL, 2 * N_HEADS * HEAD_DIM), D_MODEL),
        'w_v': dense((D_MODEL, 2 * N_HEADS * HEAD_DIM), D_MODEL),
        'k_norm': gain((HEAD_DIM,)),
        'lambda_k1': 0.1 * jax.random.normal(next(ks), (HEAD_DIM,), f32),
        'lambda_k2': 0.1 * jax.random.normal(next(ks), (HEAD_DIM,), f32),
        'attn_w_q': dense((nB, D_MODEL, 2 * N_HEADS * HEAD_DIM), D_MODEL),
        'q_norm': gain((nB, HEAD_DIM)),
        'lambda_q1': 0.1 * jax.random.normal(next(ks), (nB, HEAD_DIM), f32),
        'lambda_q2': 0.1 * jax.random.normal(next(ks), (nB, HEAD_DIM), f32),
        'sub_norm': gain((nB, 2 * HEAD_DIM)),
        'attn_w_o': dense((nB, 2 * N_HEADS * HEAD_DIM, D_MODEL), 2 * N_HEADS * HEAD_DIM),
    }


def reference(x, ffn1_norm, ffn1_w_gate, ffn1_w_up, ffn1_w_down,
              ffn2_norm, ffn2_w_gate, ffn2_w_up, ffn2_w_down, mix_norm,
              rec_w_in, rec_conv_w, rec_conv_b, rec_w_a, rec_b_a, rec_w_x, rec_b_x,
              rec_lambda, rec_w_out, kv_norm, w_k, w_v, k_norm, lambda_k1, lambda_k2,
              attn_w_q, q_norm, lambda_q1, lambda_q2, sub_norm, attn_w_o):
    S = x.shape[1]
    cos, sin = rope_tables(S)
    k_shared, v_shared = None, None
    for layer in range(DEPTH):
        if layer == N_A_LAYERS:
            k_shared, v_shared = shared_kv(x, kv_norm, w_k, w_v, k_norm, cos, sin)
        x = x + 0.5 * swiglu(rmsnorm(x, ffn1_norm[layer]), ffn1_w_gate[layer],
                             ffn1_w_up[layer], ffn1_w_down[layer])
        h = rmsnorm(x, mix_norm[layer])
        if layer < N_A_LAYERS:
            a = layer
            x = x + recurrent_block(h, rec_w_in[a], rec_conv_w[a], rec_conv_b[a], rec_w_a[a],
                                    rec_b_a[a], rec_w_x[a], rec_b_x[a], rec_lambda[a], rec_w_out[a])
        else:
            j = layer - N_A_LAYERS
            lambda_init = 0.8 - 0.6 * math.exp(-0.3 * layer)
            x = x + diff_attention(h, k_shared, v_shared, attn_w_q[j], q_norm[j], lambda_q1[j],
                                   lambda_q2[j], lambda_k1, lambda_k2, sub_norm[j], attn_w_o[j],
                                   cos, sin, lambda_init)
        x = x + 0.5 * swiglu(rmsnorm(x, ffn2_norm[layer]), ffn2_w_gate[layer],
                             ffn2_w_up[layer], ffn2_w_down[layer])
    return x
```

```python
import math
import os
from contextlib import ExitStack

import numpy as np

import concourse.bass as bass
import concourse.mybir as mybir
from concourse.bass_utils import run_bass_kernel_spmd

F32 = mybir.dt.float32
BF16 = mybir.dt.bfloat16
AF = mybir.ActivationFunctionType
ALU = mybir.AluOpType

D = 1024
S = 4096
B = 4
T = 2048
NT = 4
DFF = 2816
NF = 22
NH = 8
EPS = 1e-6
SUBEPS = 1e-5
DEPTH = 4

VEC = {}
_nv = 0


def _valloc(name, n):
    global _nv
    VEC[name] = _nv
    _nv += n


for _l in range(4):
    _valloc(("ffn1_norm", _l), 8)
    _valloc(("ffn2_norm", _l), 8)
    _valloc(("mix_norm", _l), 8)
_valloc("kv_norm", 8)
for _a in range(2):
    for _k in range(4):
        _valloc(("conv_w", _a, _k), 8)
    _valloc(("conv_b", _a), 8)
    _valloc(("b_a", _a), 8)
    _valloc(("b_x", _a), 8)
    _valloc(("lam", _a), 8)
_valloc("k_norm", 1)
_valloc(("q_norm", 0), 1)
_valloc(("q_norm", 1), 1)
_valloc(("sub_norm", 0), 1)
_valloc(("sub_norm", 1), 1)
_valloc("lk1", 1)
_valloc("lk2", 1)
_valloc(("lq1", 0), 1)
_valloc(("lq1", 1), 1)
_valloc(("lq2", 0), 1)
_valloc(("lq2", 1), 1)
_valloc("flag", 1)
_valloc("pbias", 1)
NV = _nv


class Op:
    __slots__ = ("eng", "fn", "deps", "sig", "sem", "val", "is_dma")


class Prog:
    ENG = ("pe", "act", "dve", "pool", "sp")

    def __init__(self):
        self.ops = {e: [] for e in self.ENG}
        self.res = {}

    def add(self, eng, fn, reads=(), writes=(), dma_sem=None):
        op = Op()
        op.eng = eng
        op.fn = fn
        op.deps = set()
        op.sig = False
        op.sem = dma_sem
        op.val = 0
        op.is_dma = dma_sem is not None
        for r in reads:
            st = self.res.get(r)
            if st is not None and st[0] is not None:
                op.deps.add(st[0])
        for w in writes:
            st = self.res.get(w)
            if st is not None:
                if st[0] is not None:
                    op.deps.add(st[0])
                op.deps.update(st[1])
        for r in reads:
            st = self.res.get(r)
            if st is None:
                st = [None, []]
                self.res[r] = st
            st[1].append(op)
        for w in writes:
            self.res[w] = [op, []]
        if eng == "pe" and not op.is_dma:
            op.deps = {d for d in op.deps if not (d.eng == "pe" and not d.is_dma)}
        self.ops[eng].append(op)
        return op

    def finalize(self, engsem):
        for e in self.ENG:
            for op in self.ops[e]:
                for d in op.deps:
                    d.sig = True
        cnt = {e: 0 for e in self.ENG}
        dcnt = {}
        for e in self.ENG:
            for op in self.ops[e]:
                if op.is_dma:
                    k = id(op.sem)
                    dcnt[k] = dcnt.get(k, 0) + 16
                    op.val = dcnt[k]
                elif op.sig:
                    cnt[e] += 1
                    op.val = cnt[e]
                    op.sem = engsem[e]
        return cnt

    def emit_stream(self, e, eng, tail_waits=()):
        waited = {}
        for op in self.ops[e]:
            need = {}
            for d in op.deps:
                k = id(d.sem)
                if k not in need or need[k][1] < d.val:
                    need[k] = (d.sem, d.val)
            for k, (s, v) in need.items():
                if waited.get(k, 0) < v:
                    eng.wait_ge(s, v)
                    waited[k] = v
            ins = op.fn(eng)
            if op.is_dma:
                ins.then_inc(op.sem, 16)
            elif op.sig:
                ins.then_inc(op.sem, 1)
        for (s, v) in tail_waits:
            eng.wait_ge(s, v)


class Rot:
    def __init__(self, items):
        self.items = list(items)
        self.i = 0

    def next(self):
        v = self.items[self.i % len(self.items)]
        self.i += 1
        return v


def build_program(stages=None, dump=False):
    nc = bass.Bass("TRN2", target_bir_lowering=False)
    P = Prog()
    es = ExitStack()

    shapes = {"xT_prev": [D, T], "xT_own": [D, T], "vecs": [128, NV], "cs_prev": [128, 2, T], "cs_own": [128, 2, T],
              "rotm": [128, 128], "rec_w_in": [2, D, 2 * D], "rec_w_a": [2, 8, 128, 128], "rec_w_x": [2, 8, 128, 128],
              "rec_w_out": [2, D, D], "w_k": [D, D], "w_v": [D, D], "attn_w_q": [2, D, D], "attn_w_o": [2, D, D]}
    for w in (1, 2):
        shapes[f"ffn{w}_w_gate"] = [4, D, DFF]
        shapes[f"ffn{w}_w_up"] = [4, D, DFF]
        shapes[f"ffn{w}_w_down"] = [4, DFF, D]

    class LazyDR(dict):
        def __missing__(self, name):
            ap_ = nc.dram_tensor(name, list(shapes[name]), F32, kind="ExternalInput").ap()
            self[name] = ap_
            return ap_
    dr = LazyDR()
    if stages is None:
        for nm_ in shapes:
            dr[nm_]
    outT = nc.dram_tensor("outT", [D, T], F32, kind="ExternalOutput").ap()
    kscr = nc.dram_tensor("kscr", [2, 8, 128, T], BF16, kind="Internal").ap()
    vscr = nc.dram_tensor("vscr", [2, 128, 16, D], BF16, kind="Internal").ap()

    def sb(name, shape, dt):
        return es.enter_context(nc.sbuf_tensor(name, list(shape), dt))

    def ps(name):
        return es.enter_context(nc.psum_tensor(name, [128, 512], F32))

    def sem(name):
        return es.enter_context(nc.semaphore(name))

    xT = sb("xT", [128, 8, T], F32)
    hT = sb("hT", [128, 8, T], BF16)
    arena = sb("arena", [128, 11 * T], BF16)
    NGEN = 6
    R3C = NGEN * 2048 + 2 * 2816
    r3 = sb("r3", [128, R3C], BF16)
    R4C = 6144
    r4 = sb("r4", [128, R4C], F32)
    vecs = sb("vecs_sb", [128, NV], F32)
    derived = sb("derived", [128, 64], F32)
    rotm = sb("rotm_sb", [128, 128], F32)
    ones_bf = sb("ones_bf", [128, 128], BF16)
    bones_bf = sb("bones_bf", [128, 128], BF16)
    ones_f = sb("ones_f", [128, 128], F32)
    ctail = sb("ctail", [128, 2, 8, 4], F32)
    hlast = sb("hlast", [128, 2, 8], F32)
    wax = sb("wax", [128, 4, 128], BF16)

    pbank = [ps(f"ps{i}") for i in range(8)]

    engsem = {e: sem(f"eng_{e}") for e in ("pe", "act", "dve", "pool", "sp")}
    gen_sem = [sem(f"gen{i}") for i in range(NGEN)]
    wd_sem = [sem(f"wd{i}") for i in range(2)]
    wax_sem = [sem(f"wax{i}") for i in range(4)]
    misc_sem = {n: sem(f"m_{n}") for n in ("x", "vecs", "rotm", "cs", "kst0", "kst1", "vst", "out")}
    kv_sem = {(kv, s, h): sem(f"{kv}{s}{h}") for kv in "kv" for s in range(2) for h in range(2)}

    def gen_slot(i):
        return r3[:, i * 2048:(i + 1) * 2048].rearrange("p (k f) -> p k f", k=8)

    def wd_slot(i):
        o = NGEN * 2048 + i * 2816
        return r3[:, o:o + 2816].rearrange("p (k f) -> p k f", k=11)

    def k_slot(i):
        return r3[:, i * 4096:(i + 1) * 4096]

    def v_slot(i):
        o = 8192 + i * 4096
        return r3[:, o:o + 4096].rearrange("p (b d) -> p b d", b=32)

    cs_view = r3[:, 8192:8192 + 8192].bitcast(F32).rearrange("p (a t) -> p a t", a=2)

    aT = arena[:, :].rearrange("p (f t) -> p f t", f=11)
    yT = arena[:, 0:8 * T].rearrange("p (f t) -> p f t", f=8)
    r4b = r4[:, :].bitcast(BF16)

    def R4(*blocks):
        return [("r4", b_) for b_ in blocks]

    CSKEYS = [("gen", 4), ("gen", 5), ("wd", 0), ("wd", 1)]

    def KK(si, half):
        return [("gen", 2 * si + half)]

    def VK(si, half):
        if si == 0:
            return [("gen", 4 + half)]
        return [("wd", 0)] if half == 0 else [("wd", 0), ("wd", 1)]

    dbg_extra = []
    gen_ctr = [0]
    wd_ctr = [0]
    wax_ctr = [0]
    ngen_active = [NGEN]

    def vcol(key, j=0):
        c = VEC[key] + j
        return vecs[:, c:c + 1]

    def dcol(i):
        return derived[:, i:i + 1]

    def dma(eng, out, in_, semh, reads=(), writes=()):
        return P.add(eng, lambda e, o=out, i=in_: e.dma_start(out=o, in_=i), reads=reads, writes=writes, dma_sem=semh)

    def load_gen(src_ap, ncols):
        i = gen_ctr[0] % ngen_active[0]
        gen_ctr[0] += 1
        slot = gen_slot(i)
        dma("pool", slot[:, :, 0:ncols], src_ap.rearrange("(k p) f -> p k f", p=128), gen_sem[i],
            writes=[("gen", i)])
        return i, slot

    def mm_group(bank, pairs, reads, n=512, m=128):
        def fn(e, pairs=pairs, bank=bank, n=n, m=m):
            last = None
            k = len(pairs)
            for j, (l, r) in enumerate(pairs):
                last = e.matmul(pbank[bank][0:m, 0:n], l, r, start=(j == 0), stop=(j == k - 1))
            return last
        return P.add("pe", fn, reads=reads, writes=[("ps", bank)])

    def act(out, in_, func, reads, writes, bias=None, scale=None):
        kw = {}
        if bias is not None:
            kw["bias"] = bias
        if scale is not None:
            kw["scale"] = scale
        return P.add("act", lambda e: e.activation(out, in_, func, **kw), reads=reads, writes=writes)

    def dve(fn, reads, writes):
        return P.add("dve", fn, reads=reads, writes=writes)

    psG = Rot([0, 1])
    psU = Rot([2, 3])
    psS = 4
    psD = Rot([5, 6, 7])

    dma("sp", vecs[:, :], dr["vecs"], misc_sem["vecs"], writes=["vecs"])
    dma("sp", rotm[:, :], dr["rotm"], misc_sem["rotm"], writes=["rotm"])
    dve(lambda e: e.memset(ones_bf[:, :], 1.0), [], ["ones_bf"])
    dve(lambda e: e.memset(ones_f[:, :], 1.0), [], ["ones_f"])
    dve(lambda e: e.memset(bones_bf[:, :], 0.0), [], ["bones_bf"])
    dve(lambda e: e.memset(bones_bf[0:64, 0:64], 1.0), [], ["bones_bf"])
    dve(lambda e: e.memset(bones_bf[64:128, 64:128], 1.0), [], ["bones_bf"])
    dve(lambda e: e.memset(ctail[:, :, :, :], 0.0), [], ["ctail"])
    dve(lambda e: e.memset(hlast[:, :, :], 0.0), [], ["hlast"])
    for j in range(2):
        dve(lambda e, j=j: e.tensor_scalar(vcol(("q_norm", j)), vcol(("q_norm", j)), 0.125, None, ALU.mult), ["vecs"], ["vecs"])
    for j in range(2):
        layer = 2 + j
        lam_init = 0.8 - 0.6 * math.exp(-0.3 * layer)
        fac = (1.0 - lam_init)
        dve(lambda e, j=j, fac=fac: e.tensor_scalar(vcol(("sub_norm", j)), vcol(("sub_norm", j)), fac, None, ALU.mult),
            ["vecs"], ["vecs"])
    lamv = vecs[:, VEC[("lam", 0)]:VEC[("lam", 0)] + 8]
    lamv1 = vecs[:, VEC[("lam", 1)]:VEC[("lam", 1)] + 8]
    dve(lambda e: e.tensor_copy(derived[:, 0:8], lamv), ["vecs"], ["der"])
    dve(lambda e: e.tensor_copy(derived[:, 8:16], lamv1), ["vecs"], ["der"])
    L_ = derived[:, 0:16]
    A_ = derived[:, 16:32]
    Bt = derived[:, 32:48]
    Ct = derived[:, 48:64]
    dve(lambda e: e.tensor_scalar(Bt, L_, -1.0, None, ALU.mult), ["der"], ["der"])
    dve(lambda e: e.tensor_tensor(A_, L_, Bt, ALU.max), ["der"], ["der"])
    act(A_, A_, AF.Exp, ["der"], ["der"], scale=-1.0)
    dve(lambda e: e.tensor_scalar(Bt, A_, 2.0, None, ALU.add), ["der"], ["der"])
    dve(lambda e: e.reciprocal(Bt, Bt), ["der"], ["der"])
    dve(lambda e: e.tensor_tensor(A_, A_, Bt, ALU.mult), ["der"], ["der"])
    dve(lambda e: e.tensor_tensor(Bt, A_, A_, ALU.mult), ["der"], ["der"])
    dve(lambda e: e.tensor_scalar(Ct, Bt, 1.0 / 9.0, 1.0 / 7.0, ALU.mult, ALU.add), ["der"], ["der"])
    dve(lambda e: e.tensor_tensor(Ct, Ct, Bt, ALU.mult), ["der"], ["der"])
    dve(lambda e: e.tensor_scalar(Ct, Ct, 1.0 / 5.0, None, ALU.add), ["der"], ["der"])
    dve(lambda e: e.tensor_tensor(Ct, Ct, Bt, ALU.mult), ["der"], ["der"])
    dve(lambda e: e.tensor_scalar(Ct, Ct, 1.0 / 3.0, None, ALU.add), ["der"], ["der"])
    dve(lambda e: e.tensor_tensor(Ct, Ct, Bt, ALU.mult), ["der"], ["der"])
    dve(lambda e: e.tensor_scalar(Ct, Ct, 1.0, None, ALU.add), ["der"], ["der"])
    dve(lambda e: e.tensor_tensor(Ct, Ct, A_, ALU.mult), ["der"], ["der"])
    dve(lambda e: e.tensor_scalar(Ct, Ct, 2.0, None, ALU.mult), ["der"], ["der"])
    dve(lambda e: e.tensor_scalar(Bt, L_, -1.0, 0.0, ALU.mult, ALU.max), ["der"], ["der"])
    dve(lambda e: e.tensor_tensor(Ct, Ct, Bt, ALU.add), ["der"], ["der"])
    dve(lambda e: e.tensor_scalar(L_, Ct, -4.0, None, ALU.mult), ["der"], ["der"])
    for a_ in range(2):
        for nm_ in ("b_a", "b_x"):
            c0_ = VEC[(nm_, a_)]
            dve(lambda e, c0_=c0_: e.tensor_scalar(vecs[:, c0_:c0_ + 8], vecs[:, c0_:c0_ + 8], 0.5, None, ALU.mult), ["vecs"], ["vecs"])
    CL = 0
    for j in range(2):
        layer = 2 + j
        lam_init = 0.8 - 0.6 * math.exp(-0.3 * layer)
        t0 = derived[:, 32:33]
        t1 = derived[:, 33:34]
        dve(lambda e, j=j: e.tensor_tensor(t0, vcol(("lq1", j)), vcol("lk1"), ALU.mult), ["vecs", "der"], ["der"])
        dve(lambda e, j=j: e.tensor_tensor(t1, vcol(("lq2", j)), vcol("lk2"), ALU.mult), ["vecs", "der"], ["der"])
        mm_group(psS, [(ones_f[:, :], derived[:, 32:34])], ["ones_f", "der"], n=2)
        act(derived[:, 34:36], pbank[psS][:, 0:2], AF.Exp, [("ps", psS)], ["der"])
        dve(lambda e, j=j, li=lam_init: e.scalar_tensor_tensor(
            derived[:, 16 + j:17 + j], derived[:, 35:36], -li, derived[:, 34:35], ALU.add, ALU.subtract),
            ["der"], ["der"])
    NLAM = 16
    dve(lambda e: e.memset(derived[:, 40:41], EPS), ["der"], ["der"])
    dve(lambda e: e.memset(derived[:, 41:42], SUBEPS), ["der"], ["der"])
    dve(lambda e: e.memset(derived[:, 42:43], 1.0), ["der"], ["der"])

    def eps_col(v):
        return {EPS: derived[:, 40:41], SUBEPS: derived[:, 41:42], 1.0: derived[:, 42:43]}[v]

    def rsqrt_to(out, in_, scale, eps, reads, writes, lnexp=False):
        if lnexp:
            act(out, in_, AF.Ln, reads + ["der"], writes, bias=dcol(40), scale=scale) if False else None
        act(out, in_, AF.Sqrt, reads + ["der"], writes, bias=eps_col(eps), scale=scale)
        dve(lambda e: e.reciprocal(out, out), writes, writes)

    def rmsnorm(gkey):
        sq = r4b[:, 0:4096].rearrange("p (c t) -> p c t", c=8)
        for t in range(NT):
            ts = slice(t * 512, (t + 1) * 512)
            act(sq, xT[:, :, ts], AF.Square, [("xT", c, t) for c in range(8)], R4(0, 1, 2, 3))
            mm_group(psS, [(ones_bf[:, :], sq[:, c, :]) for c in range(8)], ["ones_bf"] + R4(0, 1, 2, 3))
            rstd = r4[:, 2048 + (t % 2) * 512: 2048 + (t % 2 + 1) * 512]
            rk = ("r4", 4 + t % 2)
            rsqrt_to(rstd, pbank[psS][:, :], 1.0 / D, EPS, [("ps", psS)], [rk])
            for c in range(8):
                dve(lambda e, c=c, ts=ts, rstd=rstd: e.scalar_tensor_tensor(
                    hT[:, c, ts], xT[:, c, ts], vcol(gkey, c), rstd, ALU.mult, ALU.mult),
                    [("xT", c, t), rk, "vecs"], [("hT", t)])

    def ffn(l, which):
        rmsnorm((f"ffn{which}_norm", l))
        wg = dr[f"ffn{which}_w_gate"][l]
        wu = dr[f"ffn{which}_w_up"][l]
        wd = dr[f"ffn{which}_w_down"][l]
        for fh in range(2):
            f0 = fh * 11
            for (c0, nch) in ((0, 2), (2, 2), (4, 2), (6, 2), (8, 2), (10, 1)):
                col0 = (f0 + c0) * 128
                ig, sg = load_gen(wg[:, col0:col0 + nch * 128], nch * 128)
                iu, su = load_gen(wu[:, col0:col0 + nch * 128], nch * 128)
                for j in range(nch):
                    f = c0 + j
                    for t in range(NT):
                        ts = slice(t * 512, (t + 1) * 512)
                        bg = psG.next()
                        bu = psU.next()
                        mm_group(bg, [(sg[:, kc, j * 128:(j + 1) * 128], hT[:, kc, ts]) for kc in range(8)],
                                 [("gen", ig), ("hT", t)])
                        mm_group(bu, [(su[:, kc, j * 128:(j + 1) * 128], hT[:, kc, ts]) for kc in range(8)],
                                 [("gen", iu), ("hT", t)])
                        si = bg
                        sl = r4[:, 3072 + si * 512: 3072 + (si + 1) * 512]
                        act(sl, pbank[bg][:, :], AF.Silu, [("ps", bg)], R4(6 + si))
                        dve(lambda e, f=f, ts=ts, sl=sl, bu=bu: e.tensor_tensor(aT[:, f, ts], sl, pbank[bu][:, :], ALU.mult),
                            R4(6 + si) + [("ps", bu)], [("ar", f, t)])
            for ds in range(4):
                i = wd_ctr[0] % 2
                wd_ctr[0] += 1
                slot = wd_slot(i)
                dma("pool", slot[:, :, :],
                    wd[f0 * 128:(f0 + 11) * 128, ds * 256:(ds + 1) * 256].rearrange("(k p) d -> p k d", p=128),
                    wd_sem[i], writes=[("wd", i)])
                for dj in range(2):
                    dc = ds * 2 + dj
                    for t in range(NT):
                        ts = slice(t * 512, (t + 1) * 512)
                        b = psD.next()
                        mm_group(b, [(slot[:, fc, dj * 128:(dj + 1) * 128], aT[:, fc, ts]) for fc in range(11)],
                                 [("wd", i)] + [("ar", fc, t) for fc in range(11)])
                        dve(lambda e, dc=dc, ts=ts, b=b: e.scalar_tensor_tensor(
                            xT[:, dc, ts], pbank[b][:, :], 0.5, xT[:, dc, ts], ALU.mult, ALU.add),
                            [("ps", b), ("xT", dc, t)], [("xT", dc, t)])

    def out_proj(wsrc, src3, keyfn):
        for ds in range(4):
            i, slot = load_gen(wsrc[:, ds * 256:(ds + 1) * 256], 256)
            for dj in range(2):
                dc = ds * 2 + dj
                for t in range(NT):
                    ts = slice(t * 512, (t + 1) * 512)
                    b = psD.next()
                    mm_group(b, [(slot[:, kc, dj * 128:(dj + 1) * 128], src3[:, kc, ts]) for kc in range(8)],
                             [("gen", i)] + keyfn(t))
                    dve(lambda e, dc=dc, ts=ts, b=b: e.tensor_tensor(xT[:, dc, ts], pbank[b][:, :], xT[:, dc, ts], ALU.add),
                        [("ps", b), ("xT", dc, t)], [("xT", dc, t)])

    def rec_block(a, l, use_prev):
        rmsnorm(("mix_norm", l))
        w_in = dr["rec_w_in"][a]
        rT = r4[:, 0:2052]

        def blk(i):
            return r4[:, i * 512:(i + 1) * 512]
        gg = arena[:, 8 * T: 9 * T]
        cvb2 = [arena[:, 9 * T + i * 512: 9 * T + (i + 1) * 512] for i in range(2)]
        hs2 = [arena[:, 9 * T + 1024: 9 * T + 2048].bitcast(F32), arena[:, 10 * T: 10 * T + 1024].bitcast(F32)]
        HKS = [[("ar", 9, 2), ("ar", 9, 3)], [("ar", 10, 0), ("ar", 10, 1)]]
        u2 = [arena[:, 10 * T + 1024: 10 * T + 2048].bitcast(F32), blk(11)]
        UKS = [[("ar", 10, 2), ("ar", 10, 3)], R4(11)]
        for n in range(8):
            ig, sgt = load_gen(w_in[:, n * 128:(n + 1) * 128], 128)
            ir, srt = load_gen(w_in[:, D + n * 128: D + (n + 1) * 128], 128)
            wi = (wax_ctr[0] % 2) * 2
            wax_ctr[0] += 1
            dma("pool", wax[:, wi, :], dr["rec_w_a"][a, n], wax_sem[wi], writes=[("wax", wi)])
            dma("pool", wax[:, wi + 1, :], dr["rec_w_x"][a, n], wax_sem[wi + 1], writes=[("wax", wi + 1)])
            if use_prev:
                dve(lambda e, n=n: e.tensor_scalar(rT[:, 0:3], ctail[:, a, n, 0:3], vcol("flag"), None, ALU.mult),
                    ["ctail", "vecs"], R4(0))
                init0 = blk(11)[:, 0:1]
                dve(lambda e, n=n: e.tensor_scalar(ctail[:, a, n, 3:4], hlast[:, a, n:n + 1], vcol("flag"), None, ALU.mult),
                    ["hlast", "vecs"], [("cinit", n)])
            else:
                dve(lambda e: e.memset(rT[:, 0:3], 0.0), [], R4(0))

            def chain(t, n=n, ig=ig, ir=ir, sgt=sgt, srt=srt, wi=wi):
                ts = slice(t * 512, (t + 1) * 512)
                p = t % 2
                A_, B_, C_ = blk(5 + 3 * p), blk(6 + 3 * p), blk(7 + 3 * p)
                AK, BK, CK = R4(5 + 3 * p), R4(6 + 3 * p), R4(7 + 3 * p)
                u, UK = u2[p], UKS[p]
                hs, HK = hs2[p], HKS[p]
                cvb, CVK = cvb2[p], [("ar", 9, p)]
                st = {}
                steps = []

                def s0():
                    st["bg"] = psU.next()
                    mm_group(st["bg"], [(sgt[:, kc, 0:128], hT[:, kc, ts]) for kc in range(8)], [("gen", ig), ("hT", t)])
                steps.append(s0)
                steps.append(lambda: act(u, pbank[st["bg"]][:, :], AF.Square, [("ps", st["bg"])], UK))
                steps.append(lambda: dve(lambda e: e.tensor_scalar(u, u, 0.044715, 1.0, ALU.mult, ALU.add), UK, UK))
                steps.append(lambda: dve(lambda e: e.tensor_tensor(u, u, pbank[st["bg"]][:, :], ALU.mult), UK + [("ps", st["bg"])], UK))
                steps.append(lambda: act(u, u, AF.Tanh, UK, UK, scale=0.7978845608028654))
                steps.append(lambda: dve(lambda e: e.scalar_tensor_tensor(gg[:, ts], u, 1.0, pbank[st["bg"]][:, :], ALU.add, ALU.mult),
                                         UK + [("ps", st["bg"])], [("ar", 8, t)]))

                def s6():
                    st["br"] = psG.next()
                    mm_group(st["br"], [(srt[:, kc, 0:128], hT[:, kc, ts]) for kc in range(8)], [("gen", ir), ("hT", t)])
                steps.append(s6)
                steps.append(lambda: act(rT[:, 3 + t * 512: 3 + (t + 1) * 512], pbank[st["br"]][:, :], AF.Copy,
                                         [("ps", st["br"])], R4(t, t + 1)))
                steps.append(lambda: dve(lambda e: e.tensor_scalar(A_, rT[:, t * 512: t * 512 + 512], vcol(("conv_w", a, 0), n),
                                                                   vcol(("conv_b", a), n), ALU.mult, ALU.add),
                                         R4(t, t + 1) + ["vecs"], AK))
                for k in range(1, 4):
                    steps.append(lambda k=k: dve(lambda e: e.scalar_tensor_tensor(
                        A_, rT[:, t * 512 + k: t * 512 + k + 512], vcol(("conv_w", a, k), n), A_, ALU.mult, ALU.add),
                        R4(t, t + 1) + AK + ["vecs"], AK))
                steps.append(lambda: act(cvb, A_, AF.Copy, AK, CVK))

                def s13():
                    st["ba"], st["bx"] = psD.next(), psD.next()
                    mm_group(st["ba"], [(wax[:, wi, :], cvb)], [("wax", wi)] + CVK)
                    mm_group(st["bx"], [(wax[:, wi + 1, :], cvb)], [("wax", wi + 1)] + CVK)
                steps.append(s13)
                steps.append(lambda: act(B_, pbank[st["ba"]][:, :], AF.Tanh, [("ps", st["ba"]), "vecs"], BK, bias=vcol(("b_a", a), n), scale=0.5))
                steps.append(lambda: act(C_, pbank[st["bx"]][:, :], AF.Tanh, [("ps", st["bx"]), "vecs"], CK, bias=vcol(("b_x", a), n), scale=0.5))
                steps.append(lambda: act(B_, B_, AF.Exp, BK + ["der"], BK, scale=dcol(CL + a * 8 + n), bias=dcol(CL + a * 8 + n)))
                steps.append(lambda: dve(lambda e: e.scalar_tensor_tensor(A_, C_, 1.0, A_, ALU.add, ALU.mult), AK + CK, AK))
                steps.append(lambda: dve(lambda e: e.tensor_scalar(B_, B_, 1.0, None, ALU.min), BK, BK))
                steps.append(lambda: dve(lambda e: e.scalar_tensor_tensor(C_, B_, -1.0, B_, ALU.mult, ALU.mult), BK + CK, CK))
                steps.append(lambda: act(C_, C_, AF.Sqrt, CK + ["der"], CK, bias=eps_col(1.0), scale=1.0))
                steps.append(lambda: dve(lambda e: e.scalar_tensor_tensor(C_, C_, 0.5, A_, ALU.mult, ALU.mult), CK + AK, CK))

                def s_scan():
                    if t == 0:
                        if use_prev:
                            init, rd = ctail[:, a, n, 3:4], [("cinit", n)]
                        else:
                            init, rd = 0.0, []
                    else:
                        init, rd = hs2[(t - 1) % 2][:, 511:512], HKS[(t - 1) % 2]
                    dve(lambda e: e.tensor_tensor_scan(hs, B_, C_, init, ALU.mult, ALU.add), BK + CK + rd, HK)
                steps.append(s_scan)
                steps.append(lambda: dve(lambda e: e.scalar_tensor_tensor(yT[:, n, ts], gg[:, ts], 0.5, hs, ALU.mult, ALU.mult),
                                         [("ar", 8, t)] + HK, [("ar", n, t)]))
                return steps
            chains = [chain(t) for t in range(NT)]
            ns = len(chains[0])
            DELTA = ns // 2
            for s in range(ns + (NT - 1) * DELTA):
                for t in range(NT):
                    k = s - t * DELTA
                    if 0 <= k < ns:
                        chains[t][k]()
            if not use_prev:
                dve(lambda e, n=n: e.tensor_copy(ctail[:, a, n, 0:3], rT[:, 2048:2051]), R4(4), ["ctail"])
                dve(lambda e, n=n: e.tensor_copy(hlast[:, a, n:n + 1], hs2[1][:, 511:512]), HKS[1], ["hlast"])
        out_proj(dr["rec_w_out"][a], yT, lambda t: [("ar", kc, t) for kc in range(8)])

    def load_cs(which):
        dma("sp", cs_view, dr[which], misc_sem["cs"], writes=CSKEYS)

    def qk_head_post(b, t, gcolap, out_bf, out_key):
        ts = slice(t * 512, (t + 1) * 512)

        def tmp(i):
            return r4[:, i * 512:(i + 1) * 512]
        kg, ksq, rs, t1 = tmp(0), r4b[:, 2 * 512: 2 * 512 + 512], tmp(2), tmp(3)
        act(ksq, pbank[b][:, :], AF.Square, [("ps", b)], R4(1))
        dve(lambda e: e.tensor_scalar(kg, pbank[b][:, :], gcolap, None, ALU.mult), [("ps", b), "vecs"] + R4(1), R4(0))
        mm_group(psS, [(bones_bf[:, :], ksq)], ["bones_bf"] + R4(1))
        rsqrt_to(rs, pbank[psS][:, :], 1.0 / 64, EPS, [("ps", psS)], R4(2))
        br = psU.next()
        mm_group(br, [(rotm[:, :], kg)], ["rotm"] + R4(0))
        dve(lambda e: e.tensor_tensor(t1, pbank[br][:, :], cs_view[:, 1, ts], ALU.mult), [("ps", br)] + CSKEYS, R4(3))
        dve(lambda e: e.tensor_tensor(kg, kg, cs_view[:, 0, ts], ALU.mult), R4(0) + CSKEYS, R4(0))
        dve(lambda e: e.tensor_tensor(kg, kg, t1, ALU.add), R4(0, 3), R4(0))
        dve(lambda e: e.tensor_tensor(out_bf, kg, rs, ALU.mult), R4(0, 2), out_key)

    def kv_stage(half):
        ngen_active[0] = 4
        rmsnorm("kv_norm")
        load_cs("cs_prev" if half == 0 else "cs_own")
        ksb2 = [r4b[:, 8192 + i * 2048: 8192 + (i + 1) * 2048] for i in range(2)]
        for hp in range(4):
            i, slot = load_gen(dr["w_k"][:, hp * 256:(hp + 1) * 256], 256)
            for hj in range(2):
                h = hp * 2 + hj
                ksb = ksb2[h % 2]
                kk = R4(8 + 2 * (h % 2), 9 + 2 * (h % 2))
                for t in range(NT):
                    ts = slice(t * 512, (t + 1) * 512)
                    b = psG.next()
                    mm_group(b, [(slot[:, kc, hj * 128:(hj + 1) * 128], hT[:, kc, ts]) for kc in range(8)],
                             [("gen", i), ("hT", t)])
                    qk_head_post(b, t, vcol("k_norm"), ksb[:, ts], kk)
                dma("sp", kscr[half, h], ksb, misc_sem[f"kst{h % 2}"], reads=kk, writes=[("kscr", half, h)])
        vsb = arena[:, 0:16 * D].rearrange("p (b d) -> p b d", b=16)
        for vs in range(4):
            i, slot = load_gen(dr["w_v"][:, vs * 256:(vs + 1) * 256], 256)
            for tbp in range(8):
                b = psD.next()

                def fn(e, slot=slot, tbp=tbp, b=b):
                    last = None
                    for j in range(2):
                        tb = tbp * 2 + j
                        for kc in range(8):
                            last = e.matmul(pbank[b][:, j * 256:(j + 1) * 256], hT[:, kc, tb * 128:(tb + 1) * 128],
                                            slot[:, kc, :], start=(kc == 0), stop=(kc == 7))
                    return last
                P.add("pe", fn, reads=[("gen", i), ("hT", tbp // 2)], writes=[("ps", b)])
                act(vsb[:, tbp * 2: tbp * 2 + 2, vs * 256:(vs + 1) * 256],
                    pbank[b][:, :].rearrange("p (j d) -> p j d", j=2), AF.Copy, [("ps", b)],
                    [("ar", tbp, vs // 2), ("ar", tbp, 2 + vs // 2)])
        dma("sp", vscr[half], vsb, misc_sem["vst"], reads=[("ar", f_, t_) for f_ in range(8) for t_ in range(4)],
            writes=[("vscr", half)])
        ngen_active[0] = NGEN

    def attn_block(j, l):
        ngen_active[0] = 4
        rmsnorm(("mix_norm", l))
        load_cs("cs_own")
        qT = arena[:, 0:8 * T].rearrange("p (h t) -> p h t", h=8)
        wq = dr["attn_w_q"][j]
        for hp in range(4):
            i, slot = load_gen(wq[:, hp * 256:(hp + 1) * 256], 256)
            for hj in range(2):
                h = hp * 2 + hj
                for t in range(NT):
                    ts = slice(t * 512, (t + 1) * 512)
                    b = psG.next()
                    mm_group(b, [(slot[:, kc, hj * 128:(hj + 1) * 128], hT[:, kc, ts]) for kc in range(8)],
                             [("gen", i), ("hT", t)])
                    qk_head_post(b, t, vcol(("q_norm", j)), qT[:, h, ts], [("ar", h, t)])
        onT = hT

        def tmp(i):
            return r4[:, 2048 + i * 512: 2048 + (i + 1) * 512]
        E4 = [r4b[:, i * 1024: i * 1024 + 512] for i in range(4)]
        erot = Rot([0, 1, 2, 3])
        psSc = Rot([0, 1, 7])
        psO = Rot([2, 3])
        psL = Rot([5, 6])
        neg_lam = derived[:, NLAM + j: NLAM + j + 1]
        LOOK = 2
        tasks = []
        for h in range(NH):
            for qt in range(NT):
                for c in range(2):
                    blocks = [(0, kb, 0) for kb in range(16)] + [(1, kb, 0) for kb in range(4 * qt)] + \
                             [(1, 4 * qt + r, r) for r in range(4)]
                    for bi, (half, kb, r) in enumerate(blocks):
                        tasks.append((h, qt, c, half, kb, r, bi, len(blocks)))
        NTK = len(tasks)
        sinfo = [None] * NTK
        grp = {}

        def emit_s(idx):
            h, qt, c, half, kb, r, bi, nb = tasks[idx]
            si = h % 2
            ks, vs_ = k_slot(si), v_slot(si)
            if qt == 0 and c == 0 and bi == 0:
                for hf in range(2):
                    dma("sp", ks[:, hf * T:(hf + 1) * T], kscr[hf, h], kv_sem[("k", si, hf)],
                        reads=[("kscr", hf, h)], writes=KK(si, hf))
                for hf in range(2):
                    dma("sp", vs_[:, hf * 16:(hf + 1) * 16, :], vscr[hf][:, :, h * 128:(h + 1) * 128],
                        kv_sem[("v", si, hf)], reads=[("vscr", hf)], writes=VK(si, hf))
            pr = slice(c * 64, (c + 1) * 64)
            diag = (half == 1 and kb >= 4 * qt)
            q0 = qt * 512 + (128 * r if diag else 0)
            n = 512 - (128 * r if diag else 0)
            kcol = half * T + kb * 128
            bs = psSc.next()
            mm_group(bs, [(ks[pr, kcol:kcol + 128], qT[pr, h, q0:q0 + n])], KK(si, half) + [("ar", h, qt)], n=n)
            ei = erot.next()
            E = E4[ei]
            bias = vcol("pbias") if half == 0 else None
            act(E[:, 0:n], pbank[bs][:, 0:n], AF.Exp, [("ps", bs), "vecs"], R4(ei), bias=bias)
            if diag:
                dve(lambda e, E=E: e.memset(E[64:128, 0:64], 0.0), [], R4(ei))
            sinfo[idx] = (ei, n)

        def emit_pv(idx):
            h, qt, c, half, kb, r, bi, nb = tasks[idx]
            si = h % 2
            vs_ = v_slot(si)
            ei, n = sinfo[idx]
            E = E4[ei]
            if bi == 0:
                grp[(h, qt, c)] = (psO.next(), psL.next())
            bo, bl = grp[(h, qt, c)]
            vb = half * 16 + kb
            off = 512 - n

            def fn(e, bo=bo, bl=bl, E=E, vb=vb, n=n, off=off, first=(bi == 0), last=(bi == nb - 1), vs_=vs_):
                e.matmul(pbank[bo][:, off:off + n], vs_[:, vb, :], E[:, 0:n], start=first, stop=last)
                return e.matmul(pbank[bl][:, off:off + n], ones_bf[:, :], E[:, 0:n], start=first, stop=last)
            P.add("pe", fn, reads=R4(ei) + VK(si, half) + ["ones_bf"], writes=[("ps", bo), ("ps", bl)])
            if bi != nb - 1:
                return
            o_acc = tmp(0)
            rl = tmp(1)
            dve(lambda e, rl=rl, bl=bl: e.reciprocal(rl, pbank[bl][:, :]), [("ps", bl)], R4(5))
            if c == 0:
                dve(lambda e, rl=rl, bo=bo, o_acc=o_acc: e.tensor_tensor(o_acc, pbank[bo][:, :], rl, ALU.mult),
                    [("ps", bo)] + R4(5), R4(4))
                return
            oc = tmp(2)
            dve(lambda e, rl=rl, bo=bo, oc=oc: e.tensor_tensor(oc, pbank[bo][:, :], rl, ALU.mult),
                [("ps", bo)] + R4(5), R4(6))
            dve(lambda e, oc=oc, o_acc=o_acc: e.scalar_tensor_tensor(o_acc, oc, neg_lam, o_acc, ALU.mult, ALU.add),
                R4(6, 4) + ["der"], R4(4))
            osq = r4b[:, 2 * (2048 + 3 * 512): 2 * (2048 + 3 * 512) + 512]
            dve(lambda e, osq=osq, o_acc=o_acc: e.tensor_tensor(osq, o_acc, o_acc, ALU.mult), R4(4), R4(7))
            mm_group(psS, [(ones_bf[:, :], osq)], ["ones_bf"] + R4(7))
            rs = tmp(4)
            act(rs, pbank[psS][:, :], AF.Ln, [("ps", psS), "der"], R4(8), bias=eps_col(SUBEPS), scale=1.0 / 128)
            act(rs, rs, AF.Exp, R4(8), R4(8), scale=-0.5)
            ts = slice(qt * 512, (qt + 1) * 512)
            dve(lambda e, rs=rs, o_acc=o_acc, h=h, ts=ts: e.scalar_tensor_tensor(
                onT[:, h, ts], o_acc, vcol(("sub_norm", j)), rs, ALU.mult, ALU.mult),
                R4(4, 8) + ["vecs"], [("hT", qt)])

        for idx in range(NTK + LOOK):
            if idx < NTK:
                emit_s(idx)
            if idx - LOOK >= 0:
                emit_pv(idx - LOOK)
        ngen_active[0] = NGEN
        out_proj(dr["attn_w_o"][j], onT, lambda t: [("hT", t)])

    def load_x(which):
        dma("sp", xT[:, :, :], dr[which].rearrange("(c p) t -> p c t", p=128), misc_sem["x"],
            writes=[("xT", c, t) for c in range(8) for t in range(NT)])

    units = []
    for half in range(2):
        units.append(lambda half=half: load_x("xT_prev" if half == 0 else "xT_own"))
        for l in range(2):
            units.append(lambda l=l: ffn(l, 1))
            units.append(lambda l=l, half=half: rec_block(l, l, use_prev=(half == 1)))
            units.append(lambda l=l: ffn(l, 2))
        units.append(lambda half=half: kv_stage(half))
    for l in range(2, 4):
        units.append(lambda l=l: ffn(l, 1))
        units.append(lambda l=l: attn_block(l - 2, l))
        units.append(lambda l=l: ffn(l, 2))
    if stages is not None:
        units = [units[i] for i in stages]
    dbg_sems = []
    for ui, u in enumerate(units):
        u()
        if dump and stages is None and ui in (10, 14, 16, 17):
            dt_ = nc.dram_tensor(f"dbg{ui}", [D, T], F32, kind="ExternalOutput").ap()
            sm = sem(f"dbgsem{ui}")
            dma("sp", dt_.rearrange("(c p) t -> p c t", p=128), xT[:, :, :], sm,
                reads=[("xT", c, t) for c in range(8) for t in range(NT)], writes=[f"dbg{ui}"])
            dbg_sems.append((sm, 16))
    dma("sp", outT.rearrange("(c p) t -> p c t", p=128), xT[:, :, :], misc_sem["out"],
        reads=[("xT", c, t) for c in range(8) for t in range(NT)], writes=["out"])

    P.finalize(engsem)
    global _LAST_PROG, _LAST_DECL
    _LAST_PROG = P
    _LAST_DECL = list(dr.keys())
    out_val = 16

    with nc.Block() as block:
        @block.tensor
        def _(e):
            P.emit_stream("pe", e)

        @block.scalar
        def _(e):
            P.emit_stream("act", e)

        @block.vector
        def _(e):
            P.emit_stream("dve", e)

        @block.gpsimd
        def _(e):
            P.emit_stream("pool", e)

        @block.sync
        def _(e):
            P.emit_stream("sp", e, tail_waits=[(misc_sem["out"], out_val)] + dbg_sems + dbg_extra)

    es.close()
    return nc


def _rope_tables(pos0):
    pos = np.arange(pos0, pos0 + T, dtype=np.float32)
    inv_freq = (np.float32(10000.0) ** (-np.arange(0, 64, 2, dtype=np.float32) / np.float32(64))).astype(np.float32)
    ang = (pos[:, None] * inv_freq[None, :]).astype(np.float32)
    ang = np.concatenate([ang, ang], axis=-1)
    cos = np.cos(ang).astype(np.float32).T
    sin = np.sin(ang).astype(np.float32).T
    cs = np.stack([np.concatenate([cos, cos], 0), np.concatenate([sin, sin], 0)], axis=1)
    return np.ascontiguousarray(cs, dtype=np.float32)


def _rot_matrix():
    R = np.zeros((128, 128), np.float32)
    for m in range(128):
        if m % 64 < 32:
            R[m + 32, m] = -1.0
        else:
            R[m - 32, m] = 1.0
    return R


def _pack_vecs(inp, second_half):
    v = np.zeros((128, NV), np.float32)

    def fm(vec):
        return np.asarray(vec, np.float32).reshape(8, 128).T

    for l in range(4):
        v[:, VEC[("ffn1_norm", l)]:VEC[("ffn1_norm", l)] + 8] = fm(inp["ffn1_norm"][l])
        v[:, VEC[("ffn2_norm", l)]:VEC[("ffn2_norm", l)] + 8] = fm(inp["ffn2_norm"][l])
        v[:, VEC[("mix_norm", l)]:VEC[("mix_norm", l)] + 8] = fm(inp["mix_norm"][l])
    v[:, VEC["kv_norm"]:VEC["kv_norm"] + 8] = fm(inp["kv_norm"])
    for a in range(2):
        for k in range(4):
            v[:, VEC[("conv_w", a, k)]:VEC[("conv_w", a, k)] + 8] = fm(inp["rec_conv_w"][a, k])
        v[:, VEC[("conv_b", a)]:VEC[("conv_b", a)] + 8] = fm(inp["rec_conv_b"][a])
        v[:, VEC[("b_a", a)]:VEC[("b_a", a)] + 8] = fm(inp["rec_b_a"][a])
        v[:, VEC[("b_x", a)]:VEC[("b_x", a)] + 8] = fm(inp["rec_b_x"][a])
        v[:, VEC[("lam", a)]:VEC[("lam", a)] + 8] = fm(inp["rec_lambda"][a])

    def rep64(vec):
        return np.concatenate([np.asarray(vec, np.float32)] * 2)

    def pad64(vec):
        return np.concatenate([np.asarray(vec, np.float32), np.zeros(64, np.float32)])

    v[:, VEC["k_norm"]] = rep64(inp["k_norm"])
    for j in range(2):
        v[:, VEC[("q_norm", j)]] = rep64(inp["q_norm"][j])
        v[:, VEC[("sub_norm", j)]] = np.asarray(inp["sub_norm"][j], np.float32)
        v[:, VEC[("lq1", j)]] = pad64(inp["lambda_q1"][j])
        v[:, VEC[("lq2", j)]] = pad64(inp["lambda_q2"][j])
    v[:, VEC["lk1"]] = pad64(inp["lambda_k1"])
    v[:, VEC["lk2"]] = pad64(inp["lambda_k2"])
    v[:, VEC["flag"]] = 1.0 if second_half else 0.0
    v[:, VEC["pbias"]] = 0.0 if second_half else -30000.0
    return v


_STAGES = None
_LAST_PROG = None
_LAST_DECL = None


def make_in_maps(inp):
    x = np.asarray(inp["x"], np.float32)
    shared = {}
    for w in (1, 2):
        for nm in ("w_gate", "w_up", "w_down"):
            shared[f"ffn{w}_{nm}"] = np.ascontiguousarray(np.asarray(inp[f"ffn{w}_{nm}"], np.float32))
    for nm in ("rec_w_in", "rec_w_a", "rec_w_x", "rec_w_out", "w_k", "w_v", "attn_w_q", "attn_w_o"):
        shared[nm] = np.ascontiguousarray(np.asarray(inp[nm], np.float32))
    shared["rotm"] = _rot_matrix()
    cs0 = _rope_tables(0)
    cs1 = _rope_tables(T)
    in_maps = []
    for core in range(8):
        b, half = core // 2, core % 2
        own = np.ascontiguousarray(x[b, half * T:(half + 1) * T, :].T)
        prev = np.ascontiguousarray(x[b, 0:T, :].T)
        m = dict(shared)
        m["xT_own"] = own
        m["xT_prev"] = prev
        m["vecs"] = _pack_vecs(inp, half == 1)
        m["cs_prev"] = cs0
        m["cs_own"] = cs1 if half == 1 else cs0
        in_maps.append(m)
    return in_maps


def kernel(**inputs):
    inp = {k: np.asarray(v) for k, v in inputs.items()}
    nc = build_program(_STAGES)
    in_maps = make_in_maps(inp)
    res = run_bass_kernel_spmd(nc, in_maps, core_ids=list(range(8)))
    out = np.empty((B, S, D), np.float32)
    for core in range(8):
        b, half = core // 2, core % 2
        out[b, half * T:(half + 1) * T, :] = np.asarray(res.results[core]["outT"], np.float32).T
    return out
```

```python
import math
import os
from contextlib import ExitStack

import numpy as np

import concourse.bass as bass
import concourse.mybir as mybir
from concourse.bass_utils import run_bass_kernel_spmd

F32 = mybir.dt.float32
BF16 = mybir.dt.bfloat16
AF = mybir.ActivationFunctionType
ALU = mybir.AluOpType

D = 1024
S = 4096
B = 4
T = 2048
NT = 4
DFF = 2816
NF = 22
NH = 8
EPS = 1e-6
SUBEPS = 1e-5
DEPTH = 4

VEC = {}
_nv = 0


def _valloc(name, n):
    global _nv
    VEC[name] = _nv
    _nv += n


for _l in range(4):
    _valloc(("ffn1_norm", _l), 8)
    _valloc(("ffn2_norm", _l), 8)
    _valloc(("mix_norm", _l), 8)
_valloc("kv_norm", 8)
for _a in range(2):
    for _k in range(4):
        _valloc(("conv_w", _a, _k), 8)
    _valloc(("conv_b", _a), 8)
    _valloc(("b_a", _a), 8)
    _valloc(("b_x", _a), 8)
    _valloc(("lam", _a), 8)
_valloc("k_norm", 1)
_valloc(("q_norm", 0), 1)
_valloc(("q_norm", 1), 1)
_valloc(("sub_norm", 0), 1)
_valloc(("sub_norm", 1), 1)
_valloc("lk1", 1)
_valloc("lk2", 1)
_valloc(("lq1", 0), 1)
_valloc(("lq1", 1), 1)
_valloc(("lq2", 0), 1)
_valloc(("lq2", 1), 1)
_valloc("flag", 1)
_valloc("pbias", 1)
NV = _nv


class Op:
    __slots__ = ("eng", "fn", "deps", "sig", "sem", "val", "is_dma")


class Prog:
    ENG = ("pe", "act", "dve", "pool", "sp")

    def __init__(self):
        self.ops = {e: [] for e in self.ENG}
        self.res = {}

    def add(self, eng, fn, reads=(), writes=(), dma_sem=None):
        op = Op()
        op.eng = eng
        op.fn = fn
        op.deps = set()
        op.sig = False
        op.sem = dma_sem
        op.val = 0
        op.is_dma = dma_sem is not None
        for r in reads:
            st = self.res.get(r)
            if st is not None and st[0] is not None:
                op.deps.add(st[0])
        for w in writes:
            st = self.res.get(w)
            if st is not None:
                if st[0] is not None:
                    op.deps.add(st[0])
                op.deps.update(st[1])
        for r in reads:
            st = self.res.get(r)
            if st is None:
                st = [None, []]
                self.res[r] = st
            st[1].append(op)
        for w in writes:
            self.res[w] = [op, []]
        if eng == "pe" and not op.is_dma:
            op.deps = {d for d in op.deps if not (d.eng == "pe" and not d.is_dma)}
        self.ops[eng].append(op)
        return op

    def finalize(self, engsem):
        for e in self.ENG:
            for op in self.ops[e]:
                for d in op.deps:
                    d.sig = True
        cnt = {e: 0 for e in self.ENG}
        dcnt = {}
        for e in self.ENG:
            for op in self.ops[e]:
                if op.is_dma:
                    k = id(op.sem)
                    dcnt[k] = dcnt.get(k, 0) + 16
                    op.val = dcnt[k]
                elif op.sig:
                    cnt[e] += 1
                    op.val = cnt[e]
                    op.sem = engsem[e]
        return cnt

    def emit_stream(self, e, eng, tail_waits=()):
        waited = {}
        for op in self.ops[e]:
            need = {}
            for d in op.deps:
                k = id(d.sem)
                if k not in need or need[k][1] < d.val:
                    need[k] = (d.sem, d.val)
            for k, (s, v) in need.items():
                if waited.get(k, 0) < v:
                    eng.wait_ge(s, v)
                    waited[k] = v
            ins = op.fn(eng)
            if op.is_dma:
                ins.then_inc(op.sem, 16)
            elif op.sig:
                ins.then_inc(op.sem, 1)
        for (s, v) in tail_waits:
            eng.wait_ge(s, v)


class Rot:
    def __init__(self, items):
        self.items = list(items)
        self.i = 0

    def next(self):
        v = self.items[self.i % len(self.items)]
        self.i += 1
        return v


def build_program(stages=None, dump=False):
    nc = bass.Bass("TRN2", target_bir_lowering=False)
    P = Prog()
    es = ExitStack()

    shapes = {"xT_prev": [D, T], "xT_own": [D, T], "vecs": [128, NV], "cs_prev": [128, 2, T], "cs_own": [128, 2, T],
              "rotm": [128, 128], "rec_w_in": [2, D, 2 * D], "rec_w_a": [2, 8, 128, 128], "rec_w_x": [2, 8, 128, 128],
              "rec_w_out": [2, D, D], "w_k": [D, D], "w_v": [D, D], "attn_w_q": [2, D, D], "attn_w_o": [2, D, D]}
    for w in (1, 2):
        shapes[f"ffn{w}_w_gate"] = [4, D, DFF]
        shapes[f"ffn{w}_w_up"] = [4, D, DFF]
        shapes[f"ffn{w}_w_down"] = [4, DFF, D]

    class LazyDR(dict):
        def __missing__(self, name):
            ap_ = nc.dram_tensor(name, list(shapes[name]), F32, kind="ExternalInput").ap()
            self[name] = ap_
            return ap_
    dr = LazyDR()
    if stages is None:
        for nm_ in shapes:
            dr[nm_]
    outT = nc.dram_tensor("outT", [D, T], F32, kind="ExternalOutput").ap()
    kscr = nc.dram_tensor("kscr", [2, 8, 128, T], BF16, kind="Internal").ap()
    vscr = nc.dram_tensor("vscr", [2, 128, 16, D], BF16, kind="Internal").ap()

    def sb(name, shape, dt):
        return es.enter_context(nc.sbuf_tensor(name, list(shape), dt))

    def ps(name):
        return es.enter_context(nc.psum_tensor(name, [128, 512], F32))

    def sem(name):
        return es.enter_context(nc.semaphore(name))

    xT = sb("xT", [128, 8, T], F32)
    hT = sb("hT", [128, 8, T], BF16)
    arena = sb("arena", [128, 11 * T], BF16)
    NGEN = 6
    R3C = NGEN * 2048 + 2 * 2816
    r3 = sb("r3", [128, R3C], BF16)
    R4C = 6144
    r4 = sb("r4", [128, R4C], F32)
    vecs = sb("vecs_sb", [128, NV], F32)
    derived = sb("derived", [128, 64], F32)
    rotm = sb("rotm_sb", [128, 128], F32)
    ones_bf = sb("ones_bf", [128, 128], BF16)
    bones_bf = sb("bones_bf", [128, 128], BF16)
    ones_f = sb("ones_f", [128, 128], F32)
    ctail = sb("ctail", [128, 2, 8, 4], F32)
    hlast = sb("hlast", [128, 2, 8], F32)
    wax = sb("wax", [128, 4, 128], BF16)
    ones512 = sb("ones512", [128, 512], F32)

    pbank = [ps(f"ps{i}") for i in range(8)]

    engsem = {e: sem(f"eng_{e}") for e in ("pe", "act", "dve", "pool", "sp")}
    gen_sem = [sem(f"gen{i}") for i in range(NGEN)]
    wd_sem = [sem(f"wd{i}") for i in range(2)]
    wax_sem = [sem(f"wax{i}") for i in range(4)]
    misc_sem = {n: sem(f"m_{n}") for n in ("x", "vecs", "rotm", "cs", "kst0", "kst1", "vst", "out")}
    kv_sem = {(kv, s, h): sem(f"{kv}{s}{h}") for kv in "kv" for s in range(2) for h in range(2)}

    def gen_slot(i):
        return r3[:, i * 2048:(i + 1) * 2048].rearrange("p (k f) -> p k f", k=8)

    def wd_slot(i):
        o = NGEN * 2048 + i * 2816
        return r3[:, o:o + 2816].rearrange("p (k f) -> p k f", k=11)

    def k_slot(i):
        return r3[:, i * 4096:(i + 1) * 4096]

    def v_slot(i):
        o = 8192 + i * 4096
        return r3[:, o:o + 4096].rearrange("p (b d) -> p b d", b=32)

    cs_view = r3[:, 8192:8192 + 8192].bitcast(F32).rearrange("p (a t) -> p a t", a=2)

    aT = arena[:, :].rearrange("p (f t) -> p f t", f=11)
    yT = arena[:, 0:8 * T].rearrange("p (f t) -> p f t", f=8)
    r4b = r4[:, :].bitcast(BF16)

    def R4(*blocks):
        return [("r4", b_) for b_ in blocks]

    CSKEYS = [("gen", 4), ("gen", 5), ("wd", 0), ("wd", 1)]

    def KK(si, half):
        return [("gen", 2 * si + half)]

    def VK(si, half):
        if si == 0:
            return [("gen", 4 + half)]
        return [("wd", 0)] if half == 0 else [("wd", 0), ("wd", 1)]

    dbg_extra = []
    gen_ctr = [0]
    wd_ctr = [0]
    wax_ctr = [0]
    ngen_active = [NGEN]

    def vcol(key, j=0):
        c = VEC[key] + j
        return vecs[:, c:c + 1]

    def dcol(i):
        return derived[:, i:i + 1]

    def dma(eng, out, in_, semh, reads=(), writes=()):
        return P.add(eng, lambda e, o=out, i=in_: e.dma_start(out=o, in_=i), reads=reads, writes=writes, dma_sem=semh)

    def load_gen(src_ap, ncols):
        i = gen_ctr[0] % ngen_active[0]
        gen_ctr[0] += 1
        slot = gen_slot(i)
        dma("pool", slot[:, :, 0:ncols], src_ap.rearrange("(k p) f -> p k f", p=128), gen_sem[i],
            writes=[("gen", i)])
        return i, slot

    def mm_group(bank, pairs, reads, n=512, m=128):
        def fn(e, pairs=pairs, bank=bank, n=n, m=m):
            last = None
            k = len(pairs)
            for j, (l, r) in enumerate(pairs):
                last = e.matmul(pbank[bank][0:m, 0:n], l, r, start=(j == 0), stop=(j == k - 1))
            return last
        return P.add("pe", fn, reads=reads, writes=[("ps", bank)])

    def act(out, in_, func, reads, writes, bias=None, scale=None):
        kw = {}
        if bias is not None:
            kw["bias"] = bias
        if scale is not None:
            kw["scale"] = scale
        return P.add("act", lambda e: e.activation(out, in_, func, **kw), reads=reads, writes=writes)

    def dve(fn, reads, writes):
        return P.add("dve", fn, reads=reads, writes=writes)

    psG = Rot([0, 1])
    psU = Rot([2, 3])
    psS = 4
    psD = Rot([5, 6, 7])

    dma("sp", vecs[:, :], dr["vecs"], misc_sem["vecs"], writes=["vecs"])
    dma("sp", rotm[:, :], dr["rotm"], misc_sem["rotm"], writes=["rotm"])
    dve(lambda e: e.memset(ones_bf[:, :], 1.0), [], ["ones_bf"])
    dve(lambda e: e.memset(ones_f[:, :], 1.0), [], ["ones_f"])
    dve(lambda e: e.memset(ones512[:, :], 1.0), [], ["ones512"])
    dve(lambda e: e.memset(bones_bf[:, :], 0.0), [], ["bones_bf"])
    dve(lambda e: e.memset(bones_bf[0:64, 0:64], 1.0), [], ["bones_bf"])
    dve(lambda e: e.memset(bones_bf[64:128, 64:128], 1.0), [], ["bones_bf"])
    dve(lambda e: e.memset(ctail[:, :, :, :], 0.0), [], ["ctail"])
    dve(lambda e: e.memset(hlast[:, :, :], 0.0), [], ["hlast"])
    for j in range(2):
        dve(lambda e, j=j: e.tensor_scalar(vcol(("q_norm", j)), vcol(("q_norm", j)), 0.125, None, ALU.mult), ["vecs"], ["vecs"])
    for j in range(2):
        layer = 2 + j
        lam_init = 0.8 - 0.6 * math.exp(-0.3 * layer)
        fac = (1.0 - lam_init)
        dve(lambda e, j=j, fac=fac: e.tensor_scalar(vcol(("sub_norm", j)), vcol(("sub_norm", j)), fac, None, ALU.mult),
            ["vecs"], ["vecs"])
    lamv = vecs[:, VEC[("lam", 0)]:VEC[("lam", 0)] + 8]
    lamv1 = vecs[:, VEC[("lam", 1)]:VEC[("lam", 1)] + 8]
    dve(lambda e: e.tensor_copy(derived[:, 0:8], lamv), ["vecs"], ["der"])
    dve(lambda e: e.tensor_copy(derived[:, 8:16], lamv1), ["vecs"], ["der"])
    L_ = derived[:, 0:16]
    A_ = derived[:, 16:32]
    Bt = derived[:, 32:48]
    Ct = derived[:, 48:64]
    dve(lambda e: e.tensor_scalar(Bt, L_, -1.0, None, ALU.mult), ["der"], ["der"])
    dve(lambda e: e.tensor_tensor(A_, L_, Bt, ALU.max), ["der"], ["der"])
    act(A_, A_, AF.Exp, ["der"], ["der"], scale=-1.0)
    dve(lambda e: e.tensor_scalar(Bt, A_, 2.0, None, ALU.add), ["der"], ["der"])
    dve(lambda e: e.reciprocal(Bt, Bt), ["der"], ["der"])
    dve(lambda e: e.tensor_tensor(A_, A_, Bt, ALU.mult), ["der"], ["der"])
    dve(lambda e: e.tensor_tensor(Bt, A_, A_, ALU.mult), ["der"], ["der"])
    dve(lambda e: e.tensor_scalar(Ct, Bt, 1.0 / 9.0, 1.0 / 7.0, ALU.mult, ALU.add), ["der"], ["der"])
    dve(lambda e: e.tensor_tensor(Ct, Ct, Bt, ALU.mult), ["der"], ["der"])
    dve(lambda e: e.tensor_scalar(Ct, Ct, 1.0 / 5.0, None, ALU.add), ["der"], ["der"])
    dve(lambda e: e.tensor_tensor(Ct, Ct, Bt, ALU.mult), ["der"], ["der"])
    dve(lambda e: e.tensor_scalar(Ct, Ct, 1.0 / 3.0, None, ALU.add), ["der"], ["der"])
    dve(lambda e: e.tensor_tensor(Ct, Ct, Bt, ALU.mult), ["der"], ["der"])
    dve(lambda e: e.tensor_scalar(Ct, Ct, 1.0, None, ALU.add), ["der"], ["der"])
    dve(lambda e: e.tensor_tensor(Ct, Ct, A_, ALU.mult), ["der"], ["der"])
    dve(lambda e: e.tensor_scalar(Ct, Ct, 2.0, None, ALU.mult), ["der"], ["der"])
    dve(lambda e: e.tensor_scalar(Bt, L_, -1.0, 0.0, ALU.mult, ALU.max), ["der"], ["der"])
    dve(lambda e: e.tensor_tensor(Ct, Ct, Bt, ALU.add), ["der"], ["der"])
    dve(lambda e: e.tensor_scalar(L_, Ct, -4.0, None, ALU.mult), ["der"], ["der"])
    for a_ in range(2):
        for nm_ in ("b_a", "b_x"):
            c0_ = VEC[(nm_, a_)]
            dve(lambda e, c0_=c0_: e.tensor_scalar(vecs[:, c0_:c0_ + 8], vecs[:, c0_:c0_ + 8], 0.5, None, ALU.mult), ["vecs"], ["vecs"])
    CL = 0
    for j in range(2):
        layer = 2 + j
        lam_init = 0.8 - 0.6 * math.exp(-0.3 * layer)
        t0 = derived[:, 32:33]
        t1 = derived[:, 33:34]
        dve(lambda e, j=j: e.tensor_tensor(t0, vcol(("lq1", j)), vcol("lk1"), ALU.mult), ["vecs", "der"], ["der"])
        dve(lambda e, j=j: e.tensor_tensor(t1, vcol(("lq2", j)), vcol("lk2"), ALU.mult), ["vecs", "der"], ["der"])
        mm_group(psS, [(ones_f[:, :], derived[:, 32:34])], ["ones_f", "der"], n=2)
        act(derived[:, 34:36], pbank[psS][:, 0:2], AF.Exp, [("ps", psS)], ["der"])
        dve(lambda e, j=j, li=lam_init: e.scalar_tensor_tensor(
            derived[:, 16 + j:17 + j], derived[:, 35:36], -li, derived[:, 34:35], ALU.add, ALU.subtract),
            ["der"], ["der"])
    NLAM = 16
    dve(lambda e: e.memset(derived[:, 40:41], EPS), ["der"], ["der"])
    dve(lambda e: e.memset(derived[:, 41:42], SUBEPS), ["der"], ["der"])
    dve(lambda e: e.memset(derived[:, 42:43], 1.0), ["der"], ["der"])

    def eps_col(v):
        return {EPS: derived[:, 40:41], SUBEPS: derived[:, 41:42], 1.0: derived[:, 42:43]}[v]

    def rsqrt_to(out, in_, scale, eps, reads, writes, lnexp=False):
        if lnexp:
            act(out, in_, AF.Ln, reads + ["der"], writes, bias=dcol(40), scale=scale) if False else None
        act(out, in_, AF.Sqrt, reads + ["der"], writes, bias=eps_col(eps), scale=scale)
        dve(lambda e: e.reciprocal(out, out), writes, writes)

    def rmsnorm(gkey):
        sq = r4b[:, 0:4096].rearrange("p (c t) -> p c t", c=8)
        for t in range(NT):
            ts = slice(t * 512, (t + 1) * 512)
            act(sq, xT[:, :, ts], AF.Square, [("xT", c, t) for c in range(8)], R4(0, 1, 2, 3))
            mm_group(psS, [(ones_bf[:, :], sq[:, c, :]) for c in range(8)], ["ones_bf"] + R4(0, 1, 2, 3))
            rstd = r4[:, 2048 + (t % 2) * 512: 2048 + (t % 2 + 1) * 512]
            rk = ("r4", 4 + t % 2)
            rsqrt_to(rstd, pbank[psS][:, :], 1.0 / D, EPS, [("ps", psS)], [rk])
            for c in range(8):
                dve(lambda e, c=c, ts=ts, rstd=rstd: e.scalar_tensor_tensor(
                    hT[:, c, ts], xT[:, c, ts], vcol(gkey, c), rstd, ALU.mult, ALU.mult),
                    [("xT", c, t), rk, "vecs"], [("hT", t)])

    def ffn(l, which):
        rmsnorm((f"ffn{which}_norm", l))
        wg = dr[f"ffn{which}_w_gate"][l]
        wu = dr[f"ffn{which}_w_up"][l]
        wd = dr[f"ffn{which}_w_down"][l]
        for fh in range(2):
            f0 = fh * 11
            for (c0, nch) in ((0, 2), (2, 2), (4, 2), (6, 2), (8, 2), (10, 1)):
                col0 = (f0 + c0) * 128
                ig, sg = load_gen(wg[:, col0:col0 + nch * 128], nch * 128)
                iu, su = load_gen(wu[:, col0:col0 + nch * 128], nch * 128)
                for j in range(nch):
                    f = c0 + j
                    for t in range(NT):
                        ts = slice(t * 512, (t + 1) * 512)
                        bg = psG.next()
                        bu = psU.next()
                        mm_group(bg, [(sg[:, kc, j * 128:(j + 1) * 128], hT[:, kc, ts]) for kc in range(8)],
                                 [("gen", ig), ("hT", t)])
                        mm_group(bu, [(su[:, kc, j * 128:(j + 1) * 128], hT[:, kc, ts]) for kc in range(8)],
                                 [("gen", iu), ("hT", t)])
                        si = bg
                        sl = r4[:, 3072 + si * 512: 3072 + (si + 1) * 512]
                        act(sl, pbank[bg][:, :], AF.Silu, [("ps", bg)], R4(6 + si))
                        dve(lambda e, f=f, ts=ts, sl=sl, bu=bu: e.tensor_tensor(aT[:, f, ts], sl, pbank[bu][:, :], ALU.mult),
                            R4(6 + si) + [("ps", bu)], [("ar", f, t)])
            for ds in range(4):
                i = wd_ctr[0] % 2
                wd_ctr[0] += 1
                slot = wd_slot(i)
                dma("pool", slot[:, :, :],
                    wd[f0 * 128:(f0 + 11) * 128, ds * 256:(ds + 1) * 256].rearrange("(k p) d -> p k d", p=128),
                    wd_sem[i], writes=[("wd", i)])
                for dj in range(2):
                    dc = ds * 2 + dj
                    for t in range(NT):
                        ts = slice(t * 512, (t + 1) * 512)
                        b = psD.next()
                        mm_group(b, [(slot[:, fc, dj * 128:(dj + 1) * 128], aT[:, fc, ts]) for fc in range(11)],
                                 [("wd", i)] + [("ar", fc, t) for fc in range(11)])
                        dve(lambda e, dc=dc, ts=ts, b=b: e.scalar_tensor_tensor(
                            xT[:, dc, ts], pbank[b][:, :], 0.5, xT[:, dc, ts], ALU.mult, ALU.add),
                            [("ps", b), ("xT", dc, t)], [("xT", dc, t)])

    def out_proj(wsrc, src3, keyfn):
        for ds in range(4):
            i, slot = load_gen(wsrc[:, ds * 256:(ds + 1) * 256], 256)
            for dj in range(2):
                dc = ds * 2 + dj
                for t in range(NT):
                    ts = slice(t * 512, (t + 1) * 512)
                    b = psD.next()
                    mm_group(b, [(slot[:, kc, dj * 128:(dj + 1) * 128], src3[:, kc, ts]) for kc in range(8)],
                             [("gen", i)] + keyfn(t))
                    dve(lambda e, dc=dc, ts=ts, b=b: e.tensor_tensor(xT[:, dc, ts], pbank[b][:, :], xT[:, dc, ts], ALU.add),
                        [("ps", b), ("xT", dc, t)], [("xT", dc, t)])

    def rec_block(a, l, use_prev):
        rmsnorm(("mix_norm", l))
        w_in = dr["rec_w_in"][a]
        rT = r4[:, 0:2052]

        def blk(i):
            return r4[:, i * 512:(i + 1) * 512]
        gg = arena[:, 8 * T: 9 * T]
        cvb2 = [arena[:, 9 * T + i * 512: 9 * T + (i + 1) * 512] for i in range(2)]
        hs2 = [arena[:, 9 * T + 1024: 9 * T + 2048].bitcast(F32), arena[:, 10 * T: 10 * T + 1024].bitcast(F32)]
        HKS = [[("ar", 9, 2), ("ar", 9, 3)], [("ar", 10, 0), ("ar", 10, 1)]]
        u2 = [arena[:, 10 * T + 1024: 10 * T + 2048].bitcast(F32), blk(11)]
        UKS = [[("ar", 10, 2), ("ar", 10, 3)], R4(11)]
        for n in range(8):
            ig, sgt = load_gen(w_in[:, n * 128:(n + 1) * 128], 128)
            ir, srt = load_gen(w_in[:, D + n * 128: D + (n + 1) * 128], 128)
            wi = (wax_ctr[0] % 2) * 2
            wax_ctr[0] += 1
            dma("pool", wax[:, wi, :], dr["rec_w_a"][a, n], wax_sem[wi], writes=[("wax", wi)])
            dma("pool", wax[:, wi + 1, :], dr["rec_w_x"][a, n], wax_sem[wi + 1], writes=[("wax", wi + 1)])
            if use_prev:
                dve(lambda e, n=n: e.tensor_scalar(rT[:, 0:3], ctail[:, a, n, 0:3], vcol("flag"), None, ALU.mult),
                    ["ctail", "vecs"], R4(0))
                init0 = blk(11)[:, 0:1]
                dve(lambda e, n=n: e.tensor_scalar(ctail[:, a, n, 3:4], hlast[:, a, n:n + 1], vcol("flag"), None, ALU.mult),
                    ["hlast", "vecs"], [("cinit", n)])
            else:
                dve(lambda e: e.memset(rT[:, 0:3], 0.0), [], R4(0))

            def chain(t, n=n, ig=ig, ir=ir, sgt=sgt, srt=srt, wi=wi):
                ts = slice(t * 512, (t + 1) * 512)
                p = t % 2
                A_, B_, C_ = blk(5 + 3 * p), blk(6 + 3 * p), blk(7 + 3 * p)
                AK, BK, CK = R4(5 + 3 * p), R4(6 + 3 * p), R4(7 + 3 * p)
                u, UK = u2[p], UKS[p]
                hs, HK = hs2[p], HKS[p]
                cvb, CVK = cvb2[p], [("ar", 9, p)]
                st = {}
                steps = []

                def s0():
                    st["bg"] = psU.next()
                    mm_group(st["bg"], [(sgt[:, kc, 0:128], hT[:, kc, ts]) for kc in range(8)], [("gen", ig), ("hT", t)])
                steps.append(s0)
                steps.append(lambda: act(u, pbank[st["bg"]][:, :], AF.Square, [("ps", st["bg"])], UK))
                steps.append(lambda: dve(lambda e: e.tensor_scalar(u, u, 0.044715, 1.0, ALU.mult, ALU.add), UK, UK))
                steps.append(lambda: dve(lambda e: e.tensor_tensor(u, u, pbank[st["bg"]][:, :], ALU.mult), UK + [("ps", st["bg"])], UK))
                steps.append(lambda: act(u, u, AF.Tanh, UK, UK, scale=0.7978845608028654))
                steps.append(lambda: dve(lambda e: e.scalar_tensor_tensor(gg[:, ts], u, 1.0, pbank[st["bg"]][:, :], ALU.add, ALU.mult),
                                         UK + [("ps", st["bg"])], [("ar", 8, t)]))

                def s6():
                    st["br"] = psG.next()
                    mm_group(st["br"], [(srt[:, kc, 0:128], hT[:, kc, ts]) for kc in range(8)], [("gen", ir), ("hT", t)])
                steps.append(s6)
                steps.append(lambda: act(rT[:, 3 + t * 512: 3 + (t + 1) * 512], pbank[st["br"]][:, :], AF.Copy,
                                         [("ps", st["br"])], R4(t, t + 1)))
                steps.append(lambda: dve(lambda e: e.tensor_scalar(A_, rT[:, t * 512: t * 512 + 512], vcol(("conv_w", a, 0), n),
                                                                   vcol(("conv_b", a), n), ALU.mult, ALU.add),
                                         R4(t, t + 1) + ["vecs"], AK))
                for k in range(1, 4):
                    steps.append(lambda k=k: dve(lambda e: e.scalar_tensor_tensor(
                        A_, rT[:, t * 512 + k: t * 512 + k + 512], vcol(("conv_w", a, k), n), A_, ALU.mult, ALU.add),
                        R4(t, t + 1) + AK + ["vecs"], AK))
                steps.append(lambda: act(cvb, A_, AF.Copy, AK, CVK))

                def s13():
                    st["ba"], st["bx"] = psD.next(), psD.next()
                    mm_group(st["ba"], [(wax[:, wi, :], cvb)], [("wax", wi)] + CVK)
                    mm_group(st["bx"], [(wax[:, wi + 1, :], cvb)], [("wax", wi + 1)] + CVK)
                steps.append(s13)
                steps.append(lambda: act(B_, pbank[st["ba"]][:, :], AF.Tanh, [("ps", st["ba"]), "vecs"], BK, bias=vcol(("b_a", a), n), scale=0.5))
                steps.append(lambda: act(C_, pbank[st["bx"]][:, :], AF.Tanh, [("ps", st["bx"]), "vecs"], CK, bias=vcol(("b_x", a), n), scale=0.5))
                steps.append(lambda: act(B_, B_, AF.Exp, BK + ["der"], BK, scale=dcol(CL + a * 8 + n), bias=dcol(CL + a * 8 + n)))
                steps.append(lambda: dve(lambda e: e.scalar_tensor_tensor(A_, C_, 1.0, A_, ALU.add, ALU.mult), AK + CK, AK))
                steps.append(lambda: dve(lambda e: e.tensor_scalar(B_, B_, 1.0, None, ALU.min), BK, BK))
                steps.append(lambda: dve(lambda e: e.scalar_tensor_tensor(C_, B_, -1.0, B_, ALU.mult, ALU.mult), BK + CK, CK))
                steps.append(lambda: act(C_, C_, AF.Sqrt, CK + ["der"], CK, bias=eps_col(1.0), scale=1.0))
                steps.append(lambda: dve(lambda e: e.scalar_tensor_tensor(C_, C_, 0.5, A_, ALU.mult, ALU.mult), CK + AK, CK))

                def s_scan():
                    if t == 0:
                        if use_prev:
                            hin, rd = ctail[:, a, n, 3:4], [("cinit", n)]
                        else:
                            hin, rd = None, []
                    else:
                        hin, rd = hs2[(t - 1) % 2][:, 511:512], HKS[(t - 1) % 2]
                    dve(lambda e: e.tensor_tensor_scan(hs, B_, C_, 0.0, ALU.mult, ALU.add), BK + CK, HK)
                    if hin is not None:
                        dve(lambda e: e.tensor_tensor_scan(A_, B_, ones512[:, :], 1.0, ALU.mult, ALU.mult), BK + AK + ["ones512"], AK)
                        dve(lambda e: e.scalar_tensor_tensor(hs, A_, hin, hs, ALU.mult, ALU.add), AK + HK + rd, HK)
                steps.append(s_scan)
                steps.append(lambda: dve(lambda e: e.scalar_tensor_tensor(yT[:, n, ts], gg[:, ts], 0.5, hs, ALU.mult, ALU.mult),
                                         [("ar", 8, t)] + HK, [("ar", n, t)]))
                return steps
            chains = [chain(t) for t in range(NT)]
            ns = len(chains[0])
            DELTA = ns // 2
            for s in range(ns + (NT - 1) * DELTA):
                for t in range(NT):
                    k = s - t * DELTA
                    if 0 <= k < ns:
                        chains[t][k]()
            if not use_prev:
                dve(lambda e, n=n: e.tensor_copy(ctail[:, a, n, 0:3], rT[:, 2048:2051]), R4(4), ["ctail"])
                dve(lambda e, n=n: e.tensor_copy(hlast[:, a, n:n + 1], hs2[1][:, 511:512]), HKS[1], ["hlast"])
        out_proj(dr["rec_w_out"][a], yT, lambda t: [("ar", kc, t) for kc in range(8)])

    def load_cs(which):
        dma("sp", cs_view, dr[which], misc_sem["cs"], writes=CSKEYS)

    def qk_head_post(b, t, gcolap, out_bf, out_key):
        ts = slice(t * 512, (t + 1) * 512)

        def tmp(i):
            return r4[:, i * 512:(i + 1) * 512]
        kg, ksq, rs, t1 = tmp(0), r4b[:, 2 * 512: 2 * 512 + 512], tmp(2), tmp(3)
        act(ksq, pbank[b][:, :], AF.Square, [("ps", b)], R4(1))
        dve(lambda e: e.tensor_scalar(kg, pbank[b][:, :], gcolap, None, ALU.mult), [("ps", b), "vecs"] + R4(1), R4(0))
        mm_group(psS, [(bones_bf[:, :], ksq)], ["bones_bf"] + R4(1))
        rsqrt_to(rs, pbank[psS][:, :], 1.0 / 64, EPS, [("ps", psS)], R4(2))
        br = psU.next()
        mm_group(br, [(rotm[:, :], kg)], ["rotm"] + R4(0))
        dve(lambda e: e.tensor_tensor(t1, pbank[br][:, :], cs_view[:, 1, ts], ALU.mult), [("ps", br)] + CSKEYS, R4(3))
        dve(lambda e: e.tensor_tensor(kg, kg, cs_view[:, 0, ts], ALU.mult), R4(0) + CSKEYS, R4(0))
        dve(lambda e: e.tensor_tensor(kg, kg, t1, ALU.add), R4(0, 3), R4(0))
        dve(lambda e: e.tensor_tensor(out_bf, kg, rs, ALU.mult), R4(0, 2), out_key)

    def kv_stage(half):
        ngen_active[0] = 4
        rmsnorm("kv_norm")
        load_cs("cs_prev" if half == 0 else "cs_own")
        ksb2 = [r4b[:, 8192 + i * 2048: 8192 + (i + 1) * 2048] for i in range(2)]
        for hp in range(4):
            i, slot = load_gen(dr["w_k"][:, hp * 256:(hp + 1) * 256], 256)
            for hj in range(2):
                h = hp * 2 + hj
                ksb = ksb2[h % 2]
                kk = R4(8 + 2 * (h % 2), 9 + 2 * (h % 2))
                for t in range(NT):
                    ts = slice(t * 512, (t + 1) * 512)
                    b = psG.next()
                    mm_group(b, [(slot[:, kc, hj * 128:(hj + 1) * 128], hT[:, kc, ts]) for kc in range(8)],
                             [("gen", i), ("hT", t)])
                    qk_head_post(b, t, vcol("k_norm"), ksb[:, ts], kk)
                dma("sp", kscr[half, h], ksb, misc_sem[f"kst{h % 2}"], reads=kk, writes=[("kscr", half, h)])
        vsb = arena[:, 0:16 * D].rearrange("p (b d) -> p b d", b=16)
        for vs in range(4):
            i, slot = load_gen(dr["w_v"][:, vs * 256:(vs + 1) * 256], 256)
            for tbp in range(8):
                b = psD.next()

                def fn(e, slot=slot, tbp=tbp, b=b):
                    last = None
                    for j in range(2):
                        tb = tbp * 2 + j
                        for kc in range(8):
                            last = e.matmul(pbank[b][:, j * 256:(j + 1) * 256], hT[:, kc, tb * 128:(tb + 1) * 128],
                                            slot[:, kc, :], start=(kc == 0), stop=(kc == 7))
                    return last
                P.add("pe", fn, reads=[("gen", i), ("hT", tbp // 2)], writes=[("ps", b)])
                act(vsb[:, tbp * 2: tbp * 2 + 2, vs * 256:(vs + 1) * 256],
                    pbank[b][:, :].rearrange("p (j d) -> p j d", j=2), AF.Copy, [("ps", b)],
                    [("ar", tbp, vs // 2), ("ar", tbp, 2 + vs // 2)])
        dma("sp", vscr[half], vsb, misc_sem["vst"], reads=[("ar", f_, t_) for f_ in range(8) for t_ in range(4)],
            writes=[("vscr", half)])
        ngen_active[0] = NGEN

    def attn_block(j, l):
        ngen_active[0] = 4
        rmsnorm(("mix_norm", l))
        load_cs("cs_own")
        qT = arena[:, 0:8 * T].rearrange("p (h t) -> p h t", h=8)
        wq = dr["attn_w_q"][j]
        for hp in range(4):
            i, slot = load_gen(wq[:, hp * 256:(hp + 1) * 256], 256)
            for hj in range(2):
                h = hp * 2 + hj
                for t in range(NT):
                    ts = slice(t * 512, (t + 1) * 512)
                    b = psG.next()
                    mm_group(b, [(slot[:, kc, hj * 128:(hj + 1) * 128], hT[:, kc, ts]) for kc in range(8)],
                             [("gen", i), ("hT", t)])
                    qk_head_post(b, t, vcol(("q_norm", j)), qT[:, h, ts], [("ar", h, t)])
        onT = hT

        def tmp(i):
            return r4[:, 2048 + i * 512: 2048 + (i + 1) * 512]
        E4 = [r4b[:, i * 1024: i * 1024 + 512] for i in range(4)]
        erot = Rot([0, 1, 2, 3])
        psSc = Rot([0, 1, 7])
        psO = Rot([2, 3])
        psL = Rot([5, 6])
        neg_lam = derived[:, NLAM + j: NLAM + j + 1]
        LOOK = 2
        tasks = []
        for h in range(NH):
            for qt in range(NT):
                for c in range(2):
                    blocks = [(0, kb, 0) for kb in range(16)] + [(1, kb, 0) for kb in range(4 * qt)] + \
                             [(1, 4 * qt + r, r) for r in range(4)]
                    for bi, (half, kb, r) in enumerate(blocks):
                        tasks.append((h, qt, c, half, kb, r, bi, len(blocks)))
        NTK = len(tasks)
        sinfo = [None] * NTK
        grp = {}

        def emit_s(idx):
            h, qt, c, half, kb, r, bi, nb = tasks[idx]
            si = h % 2
            ks, vs_ = k_slot(si), v_slot(si)
            if qt == 0 and c == 0 and bi == 0:
                for hf in range(2):
                    dma("sp", ks[:, hf * T:(hf + 1) * T], kscr[hf, h], kv_sem[("k", si, hf)],
                        reads=[("kscr", hf, h)], writes=KK(si, hf))
                for hf in range(2):
                    dma("sp", vs_[:, hf * 16:(hf + 1) * 16, :], vscr[hf][:, :, h * 128:(h + 1) * 128],
                        kv_sem[("v", si, hf)], reads=[("vscr", hf)], writes=VK(si, hf))
            pr = slice(c * 64, (c + 1) * 64)
            diag = (half == 1 and kb >= 4 * qt)
            q0 = qt * 512 + (128 * r if diag else 0)
            n = 512 - (128 * r if diag else 0)
            kcol = half * T + kb * 128
            bs = psSc.next()
            mm_group(bs, [(ks[pr, kcol:kcol + 128], qT[pr, h, q0:q0 + n])], KK(si, half) + [("ar", h, qt)], n=n)
            ei = erot.next()
            E = E4[ei]
            bias = vcol("pbias") if half == 0 else None
            act(E[:, 0:n], pbank[bs][:, 0:n], AF.Exp, [("ps", bs), "vecs"], R4(ei), bias=bias)
            if diag:
                dve(lambda e, E=E: e.memset(E[64:128, 0:64], 0.0), [], R4(ei))
            sinfo[idx] = (ei, n)

        def emit_pv(idx):
            h, qt, c, half, kb, r, bi, nb = tasks[idx]
            si = h % 2
            vs_ = v_slot(si)
            ei, n = sinfo[idx]
            E = E4[ei]
            if bi == 0:
                grp[(h, qt, c)] = (psO.next(), psL.next())
            bo, bl = grp[(h, qt, c)]
            vb = half * 16 + kb
            off = 512 - n

            def fn(e, bo=bo, bl=bl, E=E, vb=vb, n=n, off=off, first=(bi == 0), last=(bi == nb - 1), vs_=vs_):
                e.matmul(pbank[bo][:, off:off + n], vs_[:, vb, :], E[:, 0:n], start=first, stop=last)
                return e.matmul(pbank[bl][:, off:off + n], ones_bf[:, :], E[:, 0:n], start=first, stop=last)
            P.add("pe", fn, reads=R4(ei) + VK(si, half) + ["ones_bf"], writes=[("ps", bo), ("ps", bl)])
            if bi != nb - 1:
                return
            o_acc = tmp(0)
            rl = tmp(1)
            dve(lambda e, rl=rl, bl=bl: e.reciprocal(rl, pbank[bl][:, :]), [("ps", bl)], R4(5))
            if c == 0:
                dve(lambda e, rl=rl, bo=bo, o_acc=o_acc: e.tensor_tensor(o_acc, pbank[bo][:, :], rl, ALU.mult),
                    [("ps", bo)] + R4(5), R4(4))
                return
            oc = tmp(2)
            dve(lambda e, rl=rl, bo=bo, oc=oc: e.tensor_tensor(oc, pbank[bo][:, :], rl, ALU.mult),
                [("ps", bo)] + R4(5), R4(6))
            dve(lambda e, oc=oc, o_acc=o_acc: e.scalar_tensor_tensor(o_acc, oc, neg_lam, o_acc, ALU.mult, ALU.add),
                R4(6, 4) + ["der"], R4(4))
            osq = r4b[:, 2 * (2048 + 3 * 512): 2 * (2048 + 3 * 512) + 512]
            dve(lambda e, osq=osq, o_acc=o_acc: e.tensor_tensor(osq, o_acc, o_acc, ALU.mult), R4(4), R4(7))
            mm_group(psS, [(ones_bf[:, :], osq)], ["ones_bf"] + R4(7))
            rs = tmp(4)
            act(rs, pbank[psS][:, :], AF.Ln, [("ps", psS), "der"], R4(8), bias=eps_col(SUBEPS), scale=1.0 / 128)
            act(rs, rs, AF.Exp, R4(8), R4(8), scale=-0.5)
            ts = slice(qt * 512, (qt + 1) * 512)
            dve(lambda e, rs=rs, o_acc=o_acc, h=h, ts=ts: e.scalar_tensor_tensor(
                onT[:, h, ts], o_acc, vcol(("sub_norm", j)), rs, ALU.mult, ALU.mult),
                R4(4, 8) + ["vecs"], [("hT", qt)])

        for idx in range(NTK + LOOK):
            if idx < NTK:
                emit_s(idx)
            if idx - LOOK >= 0:
                emit_pv(idx - LOOK)
        ngen_active[0] = NGEN
        out_proj(dr["attn_w_o"][j], onT, lambda t: [("hT", t)])

    def load_x(which):
        dma("sp", xT[:, :, :], dr[which].rearrange("(c p) t -> p c t", p=128), misc_sem["x"],
            writes=[("xT", c, t) for c in range(8) for t in range(NT)])

    units = []
    for half in range(2):
        units.append(lambda half=half: load_x("xT_prev" if half == 0 else "xT_own"))
        for l in range(2):
            units.append(lambda l=l: ffn(l, 1))
            units.append(lambda l=l, half=half: rec_block(l, l, use_prev=(half == 1)))
            units.append(lambda l=l: ffn(l, 2))
        units.append(lambda half=half: kv_stage(half))
    for l in range(2, 4):
        units.append(lambda l=l: ffn(l, 1))
        units.append(lambda l=l: attn_block(l - 2, l))
        units.append(lambda l=l: ffn(l, 2))
    if stages is not None:
        units = [units[i] for i in stages]
    dbg_sems = []
    for ui, u in enumerate(units):
        u()
        if dump and stages is None and ui in (10, 14, 16, 17):
            dt_ = nc.dram_tensor(f"dbg{ui}", [D, T], F32, kind="ExternalOutput").ap()
            sm = sem(f"dbgsem{ui}")
            dma("sp", dt_.rearrange("(c p) t -> p c t", p=128), xT[:, :, :], sm,
                reads=[("xT", c, t) for c in range(8) for t in range(NT)], writes=[f"dbg{ui}"])
            dbg_sems.append((sm, 16))
    dma("sp", outT.rearrange("(c p) t -> p c t", p=128), xT[:, :, :], misc_sem["out"],
        reads=[("xT", c, t) for c in range(8) for t in range(NT)], writes=["out"])

    P.finalize(engsem)
    global _LAST_PROG, _LAST_DECL
    _LAST_PROG = P
    _LAST_DECL = list(dr.keys())
    out_val = 16

    with nc.Block() as block:
        @block.tensor
        def _(e):
            P.emit_stream("pe", e)

        @block.scalar
        def _(e):
            P.emit_stream("act", e)

        @block.vector
        def _(e):
            P.emit_stream("dve", e)

        @block.gpsimd
        def _(e):
            P.emit_stream("pool", e)

        @block.sync
        def _(e):
            P.emit_stream("sp", e, tail_waits=[(misc_sem["out"], out_val)] + dbg_sems + dbg_extra)

    es.close()
    return nc


def _rope_tables(pos0):
    pos = np.arange(pos0, pos0 + T, dtype=np.float32)
    inv_freq = (np.float32(10000.0) ** (-np.arange(0, 64, 2, dtype=np.float32) / np.float32(64))).astype(np.float32)
    ang = (pos[:, None] * inv_freq[None, :]).astype(np.float32)
    ang = np.concatenate([ang, ang], axis=-1)
    cos = np.cos(ang).astype(np.float32).T
    sin = np.sin(ang).astype(np.float32).T
    cs = np.stack([np.concatenate([cos, cos], 0), np.concatenate([sin, sin], 0)], axis=1)
    return np.ascontiguousarray(cs, dtype=np.float32)


def _rot_matrix():
    R = np.zeros((128, 128), np.float32)
    for m in range(128):
        if m % 64 < 32:
            R[m + 32, m] = -1.0
        else:
            R[m - 32, m] = 1.0
    return R


def _pack_vecs(inp, second_half):
    v = np.zeros((128, NV), np.float32)

    def fm(vec):
        return np.asarray(vec, np.float32).reshape(8, 128).T

    for l in range(4):
        v[:, VEC[("ffn1_norm", l)]:VEC[("ffn1_norm", l)] + 8] = fm(inp["ffn1_norm"][l])
        v[:, VEC[("ffn2_norm", l)]:VEC[("ffn2_norm", l)] + 8] = fm(inp["ffn2_norm"][l])
        v[:, VEC[("mix_norm", l)]:VEC[("mix_norm", l)] + 8] = fm(inp["mix_norm"][l])
    v[:, VEC["kv_norm"]:VEC["kv_norm"] + 8] = fm(inp["kv_norm"])
    for a in range(2):
        for k in range(4):
            v[:, VEC[("conv_w", a, k)]:VEC[("conv_w", a, k)] + 8] = fm(inp["rec_conv_w"][a, k])
        v[:, VEC[("conv_b", a)]:VEC[("conv_b", a)] + 8] = fm(inp["rec_conv_b"][a])
        v[:, VEC[("b_a", a)]:VEC[("b_a", a)] + 8] = fm(inp["rec_b_a"][a])
        v[:, VEC[("b_x", a)]:VEC[("b_x", a)] + 8] = fm(inp["rec_b_x"][a])
        v[:, VEC[("lam", a)]:VEC[("lam", a)] + 8] = fm(inp["rec_lambda"][a])

    def rep64(vec):
        return np.concatenate([np.asarray(vec, np.float32)] * 2)

    def pad64(vec):
        return np.concatenate([np.asarray(vec, np.float32), np.zeros(64, np.float32)])

    v[:, VEC["k_norm"]] = rep64(inp["k_norm"])
    for j in range(2):
        v[:, VEC[("q_norm", j)]] = rep64(inp["q_norm"][j])
        v[:, VEC[("sub_norm", j)]] = np.asarray(inp["sub_norm"][j], np.float32)
        v[:, VEC[("lq1", j)]] = pad64(inp["lambda_q1"][j])
        v[:, VEC[("lq2", j)]] = pad64(inp["lambda_q2"][j])
    v[:, VEC["lk1"]] = pad64(inp["lambda_k1"])
    v[:, VEC["lk2"]] = pad64(inp["lambda_k2"])
    v[:, VEC["flag"]] = 1.0 if second_half else 0.0
    v[:, VEC["pbias"]] = 0.0 if second_half else -30000.0
    return v


_STAGES = None
_LAST_PROG = None
_LAST_DECL = None


def make_in_maps(inp):
    x = np.asarray(inp["x"], np.float32)
    shared = {}
    for w in (1, 2):
        for nm in ("w_gate", "w_up", "w_down"):
            shared[f"ffn{w}_{nm}"] = np.ascontiguousarray(np.asarray(inp[f"ffn{w}_{nm}"], np.float32))
    for nm in ("rec_w_in", "rec_w_a", "rec_w_x", "rec_w_out", "w_k", "w_v", "attn_w_q", "attn_w_o"):
        shared[nm] = np.ascontiguousarray(np.asarray(inp[nm], np.float32))
    shared["rotm"] = _rot_matrix()
    cs0 = _rope_tables(0)
    cs1 = _rope_tables(T)
    in_maps = []
    for core in range(8):
        b, half = core // 2, core % 2
        own = np.ascontiguousarray(x[b, half * T:(half + 1) * T, :].T)
        prev = np.ascontiguousarray(x[b, 0:T, :].T)
        m = dict(shared)
        m["xT_own"] = own
        m["xT_prev"] = prev
        m["vecs"] = _pack_vecs(inp, half == 1)
        m["cs_prev"] = cs0
        m["cs_own"] = cs1 if half == 1 else cs0
        in_maps.append(m)
    return in_maps


def kernel(**inputs):
    inp = {k: np.asarray(v) for k, v in inputs.items()}
    nc = build_program(_STAGES)
    in_maps = make_in_maps(inp)
    res = run_bass_kernel_spmd(nc, in_maps, core_ids=list(range(8)))
    out = np.empty((B, S, D), np.float32)
    for core in range(8):
        b, half = core // 2, core % 2
        out[b, half * T:(half + 1) * T, :] = np.asarray(res.results[core]["outT"], np.float32).T
    return out
```
